# Optimizing a Trainium2 kernel written in Bass

```python
import jax, jax.numpy as jnp
from jax import lax
import numpy as np

D_MODEL = 2048
BATCH = 1
SEQ = 16384
DEPTH = 2

HEAD_DIM = 128
N_MEM = 256
MEM_HEADS = 4
DIL_GROUPS = ((128, 1), (512, 4), (2048, 16))
HEADS_PER_GROUP = 4
DIL_HEADS = len(DIL_GROUPS) * HEADS_PER_GROUP
SB_HEADS = 12
BLOCK = 128
D_FF = 5632
CONV_WIDTH = 3
ROPE_THETA = 10000.0
EPS = 1e-6
NEG_INF = -1e30
N_A = DEPTH // 2
N_B = DEPTH - N_A

DIL_W = DIL_HEADS * HEAD_DIM
MEM_W = MEM_HEADS * HEAD_DIM
SB_W = SB_HEADS * HEAD_DIM
A_IN = 3 * DIL_W + MEM_W
B_IN = SB_W + MEM_W
A_OUT = HEADS_PER_GROUP * HEAD_DIM + MEM_W
B_OUT = SB_W + MEM_W

kernel_name = "yoco_dilated_stickbreaking_hybrid"


def _rmsnorm(x, g):
    xf = x.astype(jnp.float32)
    y = xf * lax.rsqrt(jnp.mean(xf * xf, axis=-1, keepdims=True) + EPS)
    return (y * g.astype(jnp.float32)).astype(x.dtype)


def _rope_tables(S):
    pos = jnp.arange(S, dtype=jnp.float32)
    inv = ROPE_THETA ** (-jnp.arange(0, HEAD_DIM, 2, dtype=jnp.float32) / HEAD_DIM)
    ang = pos[:, None] * inv[None, :]
    return jnp.cos(ang), jnp.sin(ang)


def _rope(x, cos, sin):
    half = x.shape[-1] // 2
    xf = x.astype(jnp.float32)
    x1, x2 = xf[..., :half], xf[..., half:]
    c, s = cos[None, :, None, :], sin[None, :, None, :]
    return jnp.concatenate([x1 * c - x2 * s, x2 * c + x1 * s], axis=-1).astype(x.dtype)


def _dilated_group(q, k, v, window, dilation):
    B, S, H, E = q.shape
    L = S // dilation
    nb = -(-L // BLOCK)
    Lp = nb * BLOCK
    span = window // dilation

    def to_blocks(t):
        t = t.reshape(B, L, dilation, H, E).transpose(0, 2, 3, 1, 4)
        t = jnp.pad(t, ((0, 0), (0, 0), (0, 0), (0, Lp - L), (0, 0)))
        return t.reshape(B, dilation, H, nb, BLOCK, E)

    def with_prev(t):
        prev = jnp.pad(t[:, :, :, :-1], ((0, 0), (0, 0), (0, 0), (1, 0), (0, 0), (0, 0)))
        return jnp.concatenate([prev, t], axis=4)

    qb, kb, vb = to_blocks(q), to_blocks(k), to_blocks(v)
    kk, vv = with_prev(kb), with_prev(vb)
    s = jnp.einsum('bdhnqe,bdhnke->bdhnqk', qb, kk,
                   preferred_element_type=jnp.float32) * (E ** -0.5)
    blk = jnp.arange(nb)[:, None, None]
    n_idx = blk * BLOCK + jnp.arange(BLOCK)[None, :, None]
    m_idx = (blk - 1) * BLOCK + jnp.arange(2 * BLOCK)[None, None, :]
    rel = n_idx - m_idx
    valid = (rel >= 0) & (rel <= span) & (m_idx >= 0)
    s = jnp.where(valid, s, NEG_INF)
    lse = jax.nn.logsumexp(s, axis=-1)
    p = jnp.exp(s - lse[..., None])
    o = jnp.einsum('bdhnqk,bdhnke->bdhnqe', p.astype(v.dtype), vv,
                   preferred_element_type=jnp.float32)
    o = o.reshape(B, dilation, H, Lp, E)[:, :, :, :L].transpose(0, 3, 1, 2, 4).reshape(B, S, H, E)
    lse = lse.reshape(B, dilation, H, Lp)[..., :L].transpose(0, 3, 1, 2).reshape(B, S, H)
    return o, lse


def _stick_breaking(q, k, v):
    B, S, H, E = q.shape
    nb = S // BLOCK
    scale = E ** -0.5
    qb = q.reshape(B, nb, BLOCK, H, E).transpose(1, 0, 3, 2, 4)
    kt = k.transpose(0, 2, 1, 3)
    vt = v.transpose(0, 2, 1, 3)
    key_pos = jnp.arange(S)

    def block(args):
        qblk, i = args
        z = jnp.einsum('bhqe,bhke->bhqk', qblk, kt,
                       preferred_element_type=jnp.float32) * scale
        q_pos = i * BLOCK + jnp.arange(BLOCK)
        causal = key_pos[None, :] < q_pos[:, None]
        log_keep = jnp.where(causal, jax.nn.log_sigmoid(-z), 0.0)
        later = lax.cumsum(log_keep, axis=3, reverse=True) - log_keep
        a = jnp.where(causal, jnp.exp(jax.nn.log_sigmoid(z) + later), 0.0)
        return jnp.einsum('bhqk,bhke->bhqe', a.astype(vt.dtype), vt,
                          preferred_element_type=jnp.float32).astype(q.dtype)

    o = lax.map(block, (qb, jnp.arange(nb)))
    return o.transpose(1, 0, 3, 2, 4).reshape(B, S, H, E)


def _memory_attention(qm, mem, g_mem, w_mem_kv):
    B, S, _ = qm.shape
    mkv = _rmsnorm(mem, g_mem) @ w_mem_kv
    mk, mv = jnp.split(mkv, 2, axis=-1)
    M = mem.shape[1]
    q = qm.reshape(B, S, MEM_HEADS, HEAD_DIM)
    mk = mk.reshape(B, M, MEM_HEADS, HEAD_DIM)
    mv = mv.reshape(B, M, MEM_HEADS, HEAD_DIM)
    s = jnp.einsum('bshe,bmhe->bhsm', q, mk, preferred_element_type=jnp.float32) * (HEAD_DIM ** -0.5)
    p = jax.nn.softmax(s, axis=-1)
    o = jnp.einsum('bhsm,bmhe->bshe', p.astype(mv.dtype), mv, preferred_element_type=jnp.float32)
    return o.astype(qm.dtype).reshape(B, S, MEM_W)


def _conv_ffn(h, w_up, w_conv, b_conv, w_down):
    S = h.shape[1]
    u = h @ w_up
    gate, val = jnp.split(u, 2, axis=-1)
    gp = jnp.pad(gate, ((0, 0), (CONV_WIDTH - 1, 0), (0, 0)))
    acc = b_conv + gp[:, CONV_WIDTH - 1:CONV_WIDTH - 1 + S] * w_conv[CONV_WIDTH - 1]
    for j in range(CONV_WIDTH - 1):
        acc = acc + gp[:, j:j + S] * w_conv[j]
    return (jax.nn.silu(acc) * val) @ w_down


def _finish_layer(x, heads, w_o, norms, w_up, w_conv, b_conv, w_down):
    x = x + _rmsnorm(heads @ w_o, norms[1])
    f = _conv_ffn(_rmsnorm(x, norms[2]), w_up, w_conv, b_conv, w_down)
    return x + _rmsnorm(f, norms[3])


def setup_inputs(seed: int = 0) -> dict:
    key = jax.random.key(seed)
    ks = jax.random.split(key, 16)
    D = D_MODEL

    def w(k, shape, fan_in):
        return jax.random.normal(k, shape, jnp.float32) * (fan_in ** -0.5)

    return {
        'x': jax.random.normal(ks[0], (BATCH, SEQ, D), jnp.float32),
        'mem': jax.random.normal(ks[1], (BATCH, N_MEM, D), jnp.float32),
        'norms': 1.0 + 0.05 * jax.random.normal(ks[2], (DEPTH, 5, D), jnp.float32),
        'w_in_a': w(ks[3], (N_A, D, A_IN), D),
        'w_o_a': w(ks[4], (N_A, A_OUT, D), A_OUT),
        'g_kv': 1.0 + 0.05 * jax.random.normal(ks[5], (D,), jnp.float32),
        'w_kv': w(ks[6], (D, 2 * SB_W), D),
        'w_in_b': w(ks[7], (N_B, D, B_IN), D),
        'w_o_b': w(ks[8], (N_B, B_OUT, D), B_OUT),
        'w_mem_kv': w(ks[9], (DEPTH, D, 2 * MEM_W), D),
        'w_up': w(ks[10], (DEPTH, D, 2 * D_FF), D),
        'w_conv': w(ks[11], (DEPTH, CONV_WIDTH, D_FF), CONV_WIDTH),
        'b_conv': 0.01 * jax.random.normal(ks[12], (DEPTH, D_FF), jnp.float32),
        'w_down': w(ks[13], (DEPTH, D_FF, D), D_FF),
    }


def reference(x, mem, norms, w_in_a, w_o_a, g_kv, w_kv, w_in_b, w_o_b,
              w_mem_kv, w_up, w_conv, b_conv, w_down):
    B, S, _ = x.shape
    cos, sin = _rope_tables(S)
    k_sh = v_sh = None
    for layer in range(DEPTH):
        nrm = norms[layer]
        if layer < N_A:
            h = _rmsnorm(x, nrm[0])
            proj = h @ w_in_a[layer]
            q, k, v, qm = jnp.split(proj, [DIL_W, 2 * DIL_W, 3 * DIL_W], axis=-1)
            q = _rope(q.reshape(B, S, DIL_HEADS, HEAD_DIM), cos, sin)
            k = _rope(k.reshape(B, S, DIL_HEADS, HEAD_DIM), cos, sin)
            v = v.reshape(B, S, DIL_HEADS, HEAD_DIM)
            outs, lses = [], []
            for gi, (window, dilation) in enumerate(DIL_GROUPS):
                sl = slice(gi * HEADS_PER_GROUP, (gi + 1) * HEADS_PER_GROUP)
                o_g, l_g = _dilated_group(q[:, :, sl], k[:, :, sl], v[:, :, sl], window, dilation)
                outs.append(o_g)
                lses.append(l_g)
            wts = jax.nn.softmax(jnp.stack(lses, axis=0), axis=0)
            o_dil = jnp.sum(wts[..., None] * jnp.stack(outs, axis=0), axis=0).astype(x.dtype)
            o_mem = _memory_attention(qm, mem, nrm[4], w_mem_kv[layer])
            heads = jnp.concatenate([o_dil.reshape(B, S, -1), o_mem], axis=-1)
            x = _finish_layer(x, heads, w_o_a[layer], nrm, w_up[layer], w_conv[layer],
                              b_conv[layer], w_down[layer])
        else:
            if layer == N_A:
                kv = _rmsnorm(x, g_kv) @ w_kv
                k_sh, v_sh = jnp.split(kv, 2, axis=-1)
                k_sh = k_sh.reshape(B, S, SB_HEADS, HEAD_DIM)
                v_sh = v_sh.reshape(B, S, SB_HEADS, HEAD_DIM)
            lb = layer - N_A
            h = _rmsnorm(x, nrm[0])
            proj = h @ w_in_b[lb]
            q, qm = jnp.split(proj, [SB_W], axis=-1)
            o_sb = _stick_breaking(q.reshape(B, S, SB_HEADS, HEAD_DIM), k_sh, v_sh)
            o_mem = _memory_attention(qm, mem, nrm[4], w_mem_kv[layer])
            heads = jnp.concatenate([o_sb.reshape(B, S, -1), o_mem], axis=-1)
            x = _finish_layer(x, heads, w_o_b[lb], nrm, w_up[layer], w_conv[layer],
                              b_conv[layer], w_down[layer])
    return x
```

```python
import numpy as np
import ml_dtypes
from contextlib import ExitStack
import concourse.bass as bass
import concourse.mybir as mybir
from concourse.bass_utils import run_bass_kernel_spmd

F32 = mybir.dt.float32
BF16 = mybir.dt.bfloat16
AF = mybir.ActivationFunctionType
ALU = mybir.AluOpType
NPBF = ml_dtypes.bfloat16

D = 2048
S = 16384
NC = 8
TOK = S // NC
HD = 128
DFF = 5632
NF = DFF // 128
NMEM = 256
EPS = 1e-6
SCALE = HD ** -0.5
TT = 512

COMPUTE = ("pe", "act", "dve", "pool")


class Op:
    __slots__ = ("eng", "fn", "reads", "writes", "lane", "idx", "waits", "signal", "sigval", "is_dma")

    def __init__(self, eng, fn, reads, writes, lane):
        self.eng = eng
        self.fn = fn
        self.reads = reads
        self.writes = writes
        self.lane = lane
        self.is_dma = lane is not None
        self.waits = []
        self.signal = False
        self.sigval = None


class Prog:
    def __init__(self, nc, same_engine_sync=True):
        self.nc = nc
        self.ops = []
        self.last_w = {}
        self.readers = {}
        self.same_engine_sync = same_engine_sync

    def op(self, eng, fn, reads=(), writes=(), lane=None):
        o = Op(eng, fn, tuple(reads), tuple(writes), lane)
        o.idx = len(self.ops)
        deps = set()
        for t in o.reads:
            w = self.last_w.get(t)
            if w is not None:
                deps.add(w)
        for t in o.writes:
            w = self.last_w.get(t)
            if w is not None:
                deps.add(w)
            for r in self.readers.get(t, ()):
                deps.add(r)
        for t in o.writes:
            self.last_w[t] = o.idx
            self.readers[t] = []
        for t in o.reads:
            self.readers.setdefault(t, []).append(o.idx)
        deps.discard(o.idx)
        o.waits = sorted(deps)
        self.ops.append(o)
        return o

    def pe(self, fn, reads=(), writes=()):
        return self.op("pe", fn, reads, writes)

    def act(self, fn, reads=(), writes=()):
        return self.op("act", fn, reads, writes)

    def dve(self, fn, reads=(), writes=()):
        return self.op("dve", fn, reads, writes)

    def pool(self, fn, reads=(), writes=()):
        return self.op("pool", fn, reads, writes)

    def dma(self, q, lane, fn, reads=(), writes=()):
        return self.op(q, fn, reads, writes, lane=lane)

    def emit(self, final_wait_ops=()):
        nc = self.nc
        ops = self.ops

        def need_sync(prod, cons):
            if prod.is_dma or cons.is_dma:
                return True
            if prod.eng != cons.eng:
                return True
            if prod.eng == "pe":
                return False
            return self.same_engine_sync

        for o in ops:
            o.waits = [w for w in o.waits if need_sync(ops[w], o)]
            for w in o.waits:
                ops[w].signal = True
        for w in final_wait_ops:
            w.signal = True
        cnt = {}
        lanes = []
        for o in ops:
            if o.is_dma:
                key = ("lane", o.lane)
                if key not in cnt:
                    lanes.append(o.lane)
                cnt[key] = cnt.get(key, 0) + 16
                o.sigval = cnt[key]
                o.signal = True
            elif o.signal:
                key = ("eng", o.eng)
                cnt[key] = cnt.get(key, 0) + 1
                o.sigval = cnt[key]
        with ExitStack() as st:
            sems = {}
            for e in COMPUTE:
                sems[("eng", e)] = st.enter_context(nc.semaphore("s_" + e))
            for i, l in enumerate(lanes):
                sems[("lane", l)] = st.enter_context(nc.semaphore("l%d" % i))
            block = st.enter_context(nc.Block())

            def semkey(o):
                return ("lane", o.lane) if o.is_dma else ("eng", o.eng)

            def run_engine(ename, eng):
                waited = {}
                for o in ops:
                    if o.eng != ename:
                        continue
                    need = {}
                    for w in o.waits:
                        p = ops[w]
                        k = semkey(p)
                        if p.sigval > need.get(k, 0):
                            need[k] = p.sigval
                    for k, v in need.items():
                        if waited.get(k, 0) >= v:
                            continue
                        eng.wait_ge(sems[k], v)
                        waited[k] = v
                    ins = o.fn(eng)
                    if o.signal:
                        ins.then_inc(sems[semkey(o)], 16 if o.is_dma else 1)
                if ename == "sp":
                    need = {}
                    for p in final_wait_ops:
                        k = semkey(p)
                        need[k] = max(need.get(k, 0), p.sigval)
                    for k, v in need.items():
                        eng.wait_ge(sems[k], v)

            @block.tensor
            def _(e):
                run_engine("pe", e)

            @block.scalar
            def _(e):
                run_engine("act", e)

            @block.vector
            def _(e):
                run_engine("dve", e)

            @block.gpsimd
            def _(e):
                run_engine("pool", e)

            @block.sync
            def _(e):
                run_engine("sp", e)


class Ctx:
    def __init__(self):
        self.nc = bass.Bass("TRN2", target_bir_lowering=False)
        self.P = Prog(self.nc)
        self.st = ExitStack()
        self.outs = []
        self.rot = {}
        self.n_psum = 0

    def din(self, name, shape, dt=F32):
        return self.nc.dram_tensor(name, list(shape), dt, kind="ExternalInput").ap()

    def dout(self, name, shape, dt=F32):
        return self.nc.dram_tensor(name, list(shape), dt, kind="ExternalOutput").ap()

    def sb(self, name, shape, dt):
        return self.st.enter_context(self.nc.sbuf_tensor(name, list(shape), dt))

    def ps(self, name, shape, dt=F32):
        return self.st.enter_context(self.nc.psum_tensor(name, list(shape), dt))

    def pool_of(self, name, n, shape, dt, psum=False):
        bufs = []
        for i in range(n):
            nm = "%s%d" % (name, i)
            bufs.append((self.ps(nm, shape, dt) if psum else self.sb(nm, shape, dt), nm))
        self.rot[name] = [bufs, 0]

        def get():
            b, i = self.rot[name]
            self.rot[name][1] = i + 1
            return b[i % len(b)]

        return get

    def store(self, lane, out_ap, in_ap, reads, q="sp"):
        o = self.P.dma(q, lane, lambda e: e.dma_start(out=out_ap, in_=in_ap), reads=reads)
        self.outs.append(o)
        return o

    def load(self, lane, out_ap, in_ap, writes, q="sp", reads=()):
        return self.P.dma(q, lane, lambda e: e.dma_start(out=out_ap, in_=in_ap), writes=writes, reads=reads)

    def mm(self, out, lhsT, rhs, start, stop, reads, writes):
        return self.P.pe(lambda e: e.matmul(out, lhsT, rhs, start=start, stop=stop), reads=reads, writes=writes)

    def tr(self, out, in_, ident, reads, writes):
        return self.P.pe(lambda e: e.transpose(out, in_, ident), reads=reads, writes=writes)

    def actv(self, out, in_, func, reads, writes, scale=1.0, bias=None):
        if bias is None:
            return self.P.act(lambda e: e.activation(out=out, in_=in_, func=func, scale=scale), reads=reads, writes=writes)
        return self.P.act(lambda e: e.activation(out=out, in_=in_, func=func, scale=scale, bias=bias), reads=reads, writes=writes)

    def tt(self, eng, out, in0, in1, op, reads, writes):
        return self.P.op(eng, lambda e: e.tensor_tensor(out=out, in0=in0, in1=in1, op=op), reads, writes)

    def ts(self, eng, out, in0, s1, s2, op0, op1, reads, writes):
        if s2 is None:
            return self.P.op(eng, lambda e: e.tensor_scalar(out=out, in0=in0, scalar1=s1, scalar2=None, op0=op0), reads, writes)
        return self.P.op(eng, lambda e: e.tensor_scalar(out=out, in0=in0, scalar1=s1, scalar2=s2, op0=op0, op1=op1), reads, writes)

    def stt(self, out, in0, scalar, in1, op0, op1, reads, writes):
        return self.P.dve(lambda e: e.scalar_tensor_tensor(out=out, in0=in0, scalar=scalar, in1=in1, op0=op0, op1=op1),
                          reads=reads, writes=writes)

    def cp(self, eng, out, in_, reads, writes):
        if eng == "act":
            return self.P.act(lambda e: e.activation(out=out, in_=in_, func=AF.Copy), reads=reads, writes=writes)
        return self.P.op(eng, lambda e: e.tensor_copy(out=out, in_=in_), reads, writes)

    def recip(self, out, in_, reads, writes):
        return self.P.dve(lambda e: e.reciprocal(out=out, in_=in_), reads=reads, writes=writes)

    def memset(self, eng, ap, val, writes):
        return self.P.op(eng, lambda e: e.memset(ap, val), (), writes)

    def finish(self):
        self.P.emit(final_wait_ops=self.outs)
        self.st.close()
        return self.nc


def setup_consts(C):
    P = C.P
    ones = C.sb("ones_bf", [128, 128], BF16)
    P.dve(lambda e: e.memset(ones[:, :], 1.0), writes=["ones"])
    C.ones = ones
    epsc = C.sb("epsc", [128, 1], F32)
    P.dve(lambda e: e.memset(epsc[:, :], EPS), writes=["epsc"])
    C.epsc = epsc
    C.sq_get = C.pool_of("sq", 3, [128, TT], BF16)
    C.rstd_get = C.pool_of("rstd", 2, [128, TT], F32)
    C.ps_ss_get = C.pool_of("ps_ss", 1, [128, TT], F32, psum=True)


def rstd_of(C, src, srctok, T, nch=16):
    P = C.P
    ps, pst = C.ps_ss_get()
    for c in range(nch):
        sq, sqt = C.sq_get()
        P.act(lambda e, c=c, sq=sq: e.activation(out=sq[:, 0:T], in_=src[:, c, 0:T], func=AF.Square),
              reads=[(srctok, c)], writes=[sqt])
        P.pe(lambda e, c=c, sq=sq: e.matmul(ps[:, 0:T], C.ones[:, :], sq[:, 0:T], start=(c == 0), stop=(c == nch - 1)),
             reads=[sqt, "ones"], writes=[pst])
    r, rt = C.rstd_get()
    P.act(lambda e: e.activation(out=r[:, 0:T], in_=ps[:, 0:T], func=AF.Sqrt, scale=1.0 / D, bias=C.epsc[:, 0:1]),
          reads=[pst, "epsc"], writes=[rt])
    P.dve(lambda e: e.reciprocal(out=r[:, 0:T], in_=r[:, 0:T]), reads=[rt], writes=[rt])
    return r, rt


def norm_apply(C, x3, xtok, r, rt, gain, gtok, h, htok, T, nch=16):
    P = C.P
    if not hasattr(C, "na_get"):
        C.na_get = C.pool_of("natmp", 2, [128, TT], F32)
    for c in range(nch):
        if c % 2 == 0:
            P.dve(lambda e, c=c: e.scalar_tensor_tensor(out=h[:, c, 0:T], in0=x3[:, c, 0:T], scalar=gain[:, c:c + 1],
                                                       in1=r[:, 0:T], op0=ALU.mult, op1=ALU.mult),
                  reads=[(xtok, c), rt, gtok], writes=[(htok, c)])
        else:
            tmp, tt_ = C.na_get()
            P.act(lambda e, c=c, tmp=tmp: e.activation(out=tmp[:, 0:T], in_=x3[:, c, 0:T], func=AF.Copy, scale=gain[:, c:c + 1]),
                  reads=[(xtok, c), gtok], writes=[tt_])
            P.pool(lambda e, c=c, tmp=tmp: e.tensor_tensor(out=h[:, c, 0:T], in0=tmp[:, 0:T], in1=r[:, 0:T], op=ALU.mult),
                   reads=[tt_, rt], writes=[(htok, c)])


def postnorm_residual(C, y, ytok, x3, xtok, gain, gtok, T):
    P = C.P
    r, rt = rstd_of(C, y, ytok, T)
    for c in range(16):
        P.dve(lambda e, c=c: e.scalar_tensor_tensor(out=y[:, c, 0:T], in0=y[:, c, 0:T], scalar=gain[:, c:c + 1],
                                                     in1=r[:, 0:T], op0=ALU.mult, op1=ALU.mult),
              reads=[(ytok, c), rt, gtok], writes=[(ytok, c)])
        P.pool(lambda e, c=c: e.tensor_tensor(out=x3[:, c, 0:T], in0=x3[:, c, 0:T], in1=y[:, c, 0:T], op=ALU.add),
               reads=[(ytok, c), (xtok, c)], writes=[(xtok, c)])


def load_x3(C, x3, xtok, src_ap, lane):
    T = src_ap.shape[2]
    C.P.dma("sp", lane, lambda e: e.dma_start(out=x3[:, :, 0:T], in_=src_ap.rearrange("c p t -> p c t")),
            writes=[(xtok, c) for c in range(16)])


def store_x3(C, dst_ap, x3, xtok, lane):
    T = dst_ap.shape[2]
    C.store(lane, dst_ap.rearrange("c p t -> p c t"), x3[:, :, 0:T], reads=[(xtok, c) for c in range(16)])


def load_small(C, name, src_ap, shape, dt=F32, q="sp"):
    t = C.sb(name, shape, dt)
    idx = tuple(slice(None) for _ in shape)
    C.P.dma(q, name, lambda e: e.dma_start(out=t[idx], in_=src_ap), writes=[name])
    return t


class WStream:
    def __init__(self, C, name, nbuf, nk, ncols):
        self.C = C
        self.name = name
        self.get = C.pool_of(name, nbuf, [128, nk, ncols], BF16)
        self.nk = nk
        self.ncols = ncols

    def fetch(self, w_ap, col0):
        t, tok = self.get()
        src = w_ap[:, :, col0:col0 + self.ncols].rearrange("c p n -> p c n")
        self.C.P.dma("pool", tok, lambda e: e.dma_start(out=t[:, :, :], in_=src), writes=[tok])
        return t, tok


def mm_fm(C, ps, pst, wt, wtok, h, htok, T, nk, tsl=None):
    P = C.P
    for c in range(nk):
        rhs = h[:, c, 0:T] if tsl is None else h[:, c, tsl]
        P.pe(lambda e, c=c, rhs=rhs: e.matmul(ps[:, 0:T], wt[:, c, :], rhs, start=(c == 0), stop=(c == nk - 1)),
             reads=[wtok, (htok, c)], writes=[pst])


def build_A0():
    C = Ctx()
    P = C.P
    NT = 2 * TOK
    xT = C.din("xT", [16, 128, NT])
    w_in = C.din("w_in", [16, 128, 5120])
    g0_d = C.din("g0", [128, 16])
    cosk = C.din("cosk", [128, NT])
    sink = C.din("sink", [128, NT])
    cosq = C.din("cosq", [128, NT])
    sinq = C.din("sinq", [128, NT])
    kT = C.dout("kT", [12, 128, NT], BF16)
    v = C.dout("v", [NT, 1536], BF16)
    qT = C.dout("qT", [12, 128, TOK], BF16)
    qmT = C.dout("qmT", [4, 128, TOK], BF16)
    setup_consts(C)
    g0 = load_small(C, "g0s", g0_d, [128, 16])
    x_get = C.pool_of("x3", 2, [128, 16, TT], F32)
    h_get = C.pool_of("hT", 2, [128, 16, TT], BF16)
    wf = WStream(C, "wf", 3, 16, 128)
    wt_ = WStream(C, "wt", 2, 16, 512)
    tab_get = C.pool_of("tab", 2, [128, 4, TT], F32)
    ps_get = C.pool_of("psA", 3, [128, TT], F32, psum=True)
    t1_get = C.pool_of("t1", 2, [128, TT], F32)
    t2_get = C.pool_of("t2", 2, [128, TT], F32)
    ko_get = C.pool_of("ko", 3, [128, TT], BF16)
    vo_get = C.pool_of("vo", 3, [128, 512], BF16)

    def rope_evac(ps, pst, tab, tabt, ci, si, dst_ap, lane):
        t1, t1t = t1_get()
        t2, t2t = t2_get()
        ko, kot = ko_get()
        P.dve(lambda e: e.tensor_tensor(out=t1[:, :], in0=ps[:, :], in1=tab[:, ci, :], op=ALU.mult),
              reads=[pst, (tabt, ci)], writes=[t1t])
        P.dve(lambda e: e.tensor_tensor(out=t2[0:64, :], in0=ps[64:128, :], in1=tab[0:64, si, :], op=ALU.mult),
              reads=[pst, (tabt, si)], writes=[(t2t, 0)])
        P.dve(lambda e: e.tensor_tensor(out=t2[64:128, :], in0=ps[0:64, :], in1=tab[64:128, si, :], op=ALU.mult),
              reads=[pst, (tabt, si)], writes=[(t2t, 1)])
        P.pool(lambda e: e.tensor_tensor(out=ko[:, :], in0=t1[:, :], in1=t2[:, :], op=ALU.add),
               reads=[t1t, (t2t, 0), (t2t, 1)], writes=[kot])
        C.store(lane, dst_ap, ko[:, :], reads=[kot])

    for ti in range(NT // TT):
        t0 = ti * TT
        own = t0 >= TOK
        x3, xt = x_get()
        load_x3(C, x3, xt, xT[:, :, t0:t0 + TT], xt)
        tab, tabt = tab_get()
        for i, src in enumerate((cosk, sink, cosq, sinq)):
            if i >= 2 and not own:
                continue
            C.load(tabt + "_%d" % i, tab[:, i, :], src[:, t0:t0 + TT], writes=[(tabt, i)])
        r, rt = rstd_of(C, x3, xt, TT)
        h, ht = h_get()
        norm_apply(C, x3, xt, r, rt, g0, "g0s", h, ht, TT)
        for hh in range(12):
            wtile, wtok = wf.fetch(w_in, 1536 + 128 * hh)
            ps, pst = ps_get()
            mm_fm(C, ps, pst, wtile, wtok, h, ht, TT, 16)
            rope_evac(ps, pst, tab, tabt, 0, 1, kT[hh, :, t0:t0 + TT], "st_k")
        for g in range(3):
            wtile, wtok = wt_.fetch(w_in, 3072 + 512 * g)
            for b in range(TT // 128):
                ps, pst = ps_get()
                for c in range(16):
                    C.mm(ps[:, :], h[:, c, 128 * b:128 * b + 128], wtile[:, c, :], c == 0, c == 15,
                         reads=[wtok, (ht, c)], writes=[pst])
                vo, vot = vo_get()
                P.act(lambda e, ps=ps, vo=vo: e.activation(out=vo[:, :], in_=ps[:, :], func=AF.Copy),
                      reads=[pst], writes=[vot])
                C.store("st_v", v[t0 + 128 * b:t0 + 128 * b + 128, 512 * g:512 * g + 512], vo[:, :], reads=[vot])
        if own:
            for hh in range(12):
                wtile, wtok = wf.fetch(w_in, 128 * hh)
                ps, pst = ps_get()
                mm_fm(C, ps, pst, wtile, wtok, h, ht, TT, 16)
                rope_evac(ps, pst, tab, tabt, 2, 3, qT[hh, :, t0 - TOK:t0 - TOK + TT], "st_q")
            for hh in range(4):
                wtile, wtok = wf.fetch(w_in, 4608 + 128 * hh)
                ps, pst = ps_get()
                mm_fm(C, ps, pst, wtile, wtok, h, ht, TT, 16)
                ko, kot = ko_get()
                P.act(lambda e, ps=ps, ko=ko: e.activation(out=ko[:, :], in_=ps[:, :], func=AF.Copy, scale=SCALE),
                      reads=[pst], writes=[kot])
                C.store("st_qm", qmT[hh, :, t0 - TOK:t0 - TOK + TT], ko[:, :], reads=[kot])
    return C.finish()


def mem_kv(C, memT_d, g4_d, wmkv_d):
    C.xo_get = C.pool_of("xo3", 1, [128, 16, TT], F32)
    m3, m3t = C.xo_get()
    load_x3(C, m3, m3t, memT_d, "m3")
    g4 = load_small(C, "g4s", g4_d, [128, 16])
    r, rt = rstd_of(C, m3, m3t, NMEM)
    mh = C.sb("mh", [128, 16, NMEM], BF16)
    norm_apply(C, m3, m3t, r, rt, g4, "g4s", mh, "mh", NMEM)
    mkT = C.sb("mkT", [128, 4, NMEM], BF16)
    mv = C.sb("mv", [128, 2, 512], BF16)
    wf = WStream(C, "wmf", 2, 16, 128)
    wt_ = WStream(C, "wmt", 1, 16, 512)
    psm_get = C.pool_of("psM", 2, [128, TT], F32, psum=True)
    C.psm_get = psm_get
    for hh in range(4):
        wtile, wtok = wf.fetch(wmkv_d, 128 * hh)
        ps, pst = psm_get()
        mm_fm(C, ps, pst, wtile, wtok, mh, "mh", NMEM, 16)
        C.cp("act", mkT[:, hh, :], ps[:, 0:NMEM], reads=[pst], writes=["mkT"])
    wtile, wtok = wt_.fetch(wmkv_d, 512)
    for b in range(2):
        ps, pst = psm_get()
        for c in range(16):
            C.mm(ps[:, :], mh[:, c, 128 * b:128 * b + 128], wtile[:, c, :], c == 0, c == 15,
                 reads=[wtok, ("mh", c)], writes=[pst])
        C.cp("act", mv[:, b, :], ps[:, :], reads=[pst], writes=["mv"])
    return mkT, mv


def mem_attn(C, mkT, mv, qmT_d, heads, htok, hc0):
    qm_get = C.pool_of("qmt", 2, [128, TT], BF16)
    pm_get = C.pool_of("pm", 2, [128, 2, TT], BF16)
    rd_get = C.pool_of("rdm", 2, [128, TT], F32)
    for ti in range(TOK // TT):
        t0 = ti * TT
        for hh in range(4):
            qm, qmt = qm_get()
            C.load(qmt, qm[:, :], qmT_d[hh, :, t0:t0 + TT], writes=[qmt])
            pm, pmt = pm_get()
            for mc in range(2):
                ps, pst = C.psm_get()
                C.mm(ps[:, :], mkT[:, hh, 128 * mc:128 * mc + 128], qm[:, :], True, True, reads=["mkT", qmt], writes=[pst])
                C.actv(pm[:, mc, :], ps[:, :], AF.Exp, reads=[pst], writes=[(pmt, mc)])
            pso, psot = C.psm_get()
            for mc in range(2):
                C.mm(pso[:, :], mv[:, mc, 128 * hh:128 * hh + 128], pm[:, mc, :], mc == 0, mc == 1,
                     reads=["mv", (pmt, mc)], writes=[psot])
            psd, psdt = C.psm_get()
            for mc in range(2):
                C.mm(psd[:, :], C.ones[:, :], pm[:, mc, :], mc == 0, mc == 1, reads=["ones", (pmt, mc)], writes=[psdt])
            rd, rdt = rd_get()
            C.recip(rd[:, :], psd[:, :], reads=[psdt], writes=[rdt])
            C.tt("dve", heads[:, hc0 + hh, t0:t0 + TT], pso[:, :], rd[:, :], ALU.mult, reads=[psot, rdt],
                 writes=[(htok, hc0 + hh, ti)])


def out_proj(C, heads, htok, nhc, wo_d, x_d, g1_d, xo_d):
    g1 = load_small(C, "g1s", g1_d, [128, 16])
    wos = WStream(C, "wos", 3, nhc, 128)
    y_get = C.pool_of("yT", 1, [128, 16, TT], F32)
    for ti in range(TOK // TT):
        t0 = ti * TT
        x3, xt = C.xo_get()
        load_x3(C, x3, xt, x_d[:, :, t0:t0 + TT], xt)
        y, yt = y_get()
        for dc in range(16):
            ps, pst = C.psm_get()
            wo, wot = wos.fetch(wo_d, 128 * dc)
            for hc in range(nhc):
                C.mm(ps[:, :], wo[:, hc, :], heads[:, hc, t0:t0 + TT], hc == 0, hc == nhc - 1,
                     reads=[wot, (htok, hc, ti)], writes=[pst])
            C.cp("act", y[:, dc, :], ps[:, :], reads=[pst], writes=[(yt, dc)])
        postnorm_residual(C, y, yt, x3, xt, g1, "g1s", TT)
        store_x3(C, xo_d[:, :, t0:t0 + TT], x3, xt, "st_xo")


def build_A1():
    C = Ctx()
    P = C.P
    NT = 2 * TOK
    qT = C.din("qT", [12, 128, TOK], BF16)
    kT = C.din("kT", [12, 128, NT], BF16)
    v = C.din("v", [NT, 1536], BF16)
    qmT = C.din("qmT", [4, 128, TOK], BF16)
    xT = C.din("xT", [16, 128, TOK])
    memT = C.din("memT", [16, 128, NMEM])
    g1_d = C.din("g1", [128, 16])
    g4_d = C.din("g4", [128, 16])
    wmkv = C.din("wmkv", [16, 128, 1024])
    wo_d = C.din("wo", [8, 128, D])
    masks_d = C.din("masks", [128, 3, 256], BF16)
    xo = C.dout("xo", [16, 128, TOK])
    setup_consts(C)
    masks = load_small(C, "masks_s", masks_d, [128, 3, 256], BF16)
    mkT, mv = mem_kv(C, memT, g4_d, wmkv)
    heads = C.sb("heads", [128, 8, TOK], BF16)
    num = C.sb("num", [128, TOK], F32)
    den = C.sb("den", [128, TOK], F32)
    k_get = C.pool_of("kbuf", 2, [128, NT], BF16)
    q_get = C.pool_of("qbuf", 2, [128, TOK], BF16)
    v_get = C.pool_of("vbuf", 4, [128, 2, 128], BF16)
    e_get = C.pool_of("ebuf", 3, [128, 256], BF16)
    pss_get = C.pool_of("psS", 2, [128, 256], F32, psum=True)
    pso_get = C.pool_of("psO", 2, [128, 256], F32, psum=True)
    for j in range(4):
        for g, d in enumerate((1, 4, 16)):
            hh = 4 * g + j
            kb, kbt = k_get()
            C.load(kbt, kb[:, :], kT[hh, :, :], writes=[kbt])
            qb, qbt = q_get()
            C.load(qbt, qb[:, :], qT[hh, :, :], writes=[qbt])
            for n in range(16 // d):
                for r in range(d):
                    bq = 128 * d * n + r
                    qs = slice(bq, bq + 127 * d + 1, d)
                    kc = slice(TOK + bq, TOK + bq + 127 * d + 1, d)
                    kp = slice(TOK + bq - 128 * d, TOK + bq - 128 * d + 127 * d + 1, d)
                    vb, vbt = v_get()
                    C.load(vbt + "p", vb[:, 0, :], v[kp, 128 * hh:128 * hh + 128], writes=[(vbt, 0)])
                    C.load(vbt + "c", vb[:, 1, :], v[kc, 128 * hh:128 * hh + 128], writes=[(vbt, 1)])
                    ps, pst = pss_get()
                    C.mm(ps[:, 0:128], kb[:, kp], qb[:, qs], True, True, reads=[kbt, qbt], writes=[(pst, 0)])
                    C.mm(ps[:, 128:256], kb[:, kc], qb[:, qs], True, True, reads=[kbt, qbt], writes=[(pst, 1)])
                    eb, ebt = e_get()
                    C.actv(eb[:, :], ps[:, :], AF.Exp, reads=[(pst, 0), (pst, 1)], writes=[ebt])
                    mi = 1 if n == 0 else 0
                    C.tt("dve", eb[:, :], eb[:, :], masks[:, mi, :], ALU.mult, reads=[ebt, "masks_s"], writes=[ebt])
                    po, pot = pso_get()
                    C.mm(po[:, 0:128], vb[:, 0, :], eb[:, 0:128], True, False, reads=[(vbt, 0), ebt], writes=[(pot, 0)])
                    C.mm(po[:, 0:128], vb[:, 1, :], eb[:, 128:256], False, True, reads=[(vbt, 1), ebt], writes=[(pot, 0)])
                    C.mm(po[:, 128:256], C.ones[:, :], eb[:, 0:128], True, False, reads=["ones", ebt], writes=[(pot, 1)])
                    C.mm(po[:, 128:256], C.ones[:, :], eb[:, 128:256], False, True, reads=["ones", ebt], writes=[(pot, 1)])
                    ut = ("nd", n, r) if d == 1 else "nd_all"
                    if g == 0:
                        C.cp("dve", num[:, qs], po[:, 0:128], reads=[(pot, 0)], writes=["num"])
                        C.cp("dve", den[:, qs], po[:, 128:256], reads=[(pot, 1)], writes=["den"])
                    else:
                        C.tt("dve", num[:, qs], num[:, qs], po[:, 0:128], ALU.add, reads=[(pot, 0), "num"], writes=["num"])
                        C.tt("dve", den[:, qs], den[:, qs], po[:, 128:256], ALU.add, reads=[(pot, 1), "den"], writes=["den"])
        C.recip(den[:, :], den[:, :], reads=["den"], writes=["den"])
        for ti in range(TOK // TT):
            C.tt("dve", heads[:, j, ti * TT:(ti + 1) * TT], num[:, ti * TT:(ti + 1) * TT], den[:, ti * TT:(ti + 1) * TT], ALU.mult,
                 reads=["num", "den"], writes=[("heads", j, ti)])
    mem_attn(C, mkT, mv, qmT, heads, "heads", 4)
    out_proj(C, heads, "heads", 8, wo_d, xT, g1_d, xo)
    return C.finish()


def build_F():
    C = Ctx()
    P = C.P
    xT = C.din("xT", [16, 128, TOK])
    xh = C.din("xh", [16, 128, 2])
    g2_d = C.din("g2", [128, 16])
    g3_d = C.din("g3", [128, 16])
    wup = C.din("wup", [16, 128, 2 * DFF])
    wcv_d = C.din("wcv", [128, NF, 3])
    bcv_d = C.din("bcv", [128, NF])
    wdn = C.din("wdn", [NF, 128, D])
    xo = C.dout("xo", [16, 128, TOK])
    setup_consts(C)
    g2 = load_small(C, "g2s", g2_d, [128, 16])
    g3 = load_small(C, "g3s", g3_d, [128, 16])
    wcv = load_small(C, "wcvs", wcv_d, [128, NF, 3])
    bcv = load_small(C, "bcvs", bcv_d, [128, NF])
    x_get = C.pool_of("x3", 1, [128, 16, TT], F32)
    h = C.sb("hT", [128, 16, TT], BF16)
    y = C.sb("yT", [128, 16, TT], F32)
    gT = C.sb("gT", [128, NF, TT], BF16)
    xh3 = C.sb("xh3", [128, 16, 2], F32)
    hh = C.sb("hh", [128, 16, 2], BF16)
    gprev = C.sb("gprev", [128, NF, 2], F32)
    wu = WStream(C, "wu", 4, 16, 128)
    wd = WStream(C, "wd", 2, NF, 128)
    psg_get = C.pool_of("psG", 2, [128, TT], F32, psum=True)
    psv_get = C.pool_of("psV", 2, [128, TT], F32, psum=True)
    psh_get = C.pool_of("psH", 1, [128, 2], F32, psum=True)
    psd_get = C.pool_of("psD", 2, [128, TT], F32, psum=True)
    gb_get = C.pool_of("gbuf", 2, [128, TT + 2], F32)
    acc_get = C.pool_of("acc", 2, [128, TT], F32)
    sl_get = C.pool_of("silu", 2, [128, TT], F32)
    load_x3(C, xh3, "xh3", xh, "xh3")
    r, rt = rstd_of(C, xh3, "xh3", 2)
    norm_apply(C, xh3, "xh3", r, rt, g2, "g2s", hh, "hh", 2)
    for ti in range(TOK // TT):
        t0 = ti * TT
        x3, xt = x_get()
        load_x3(C, x3, xt, xT[:, :, t0:t0 + TT], xt)
        r, rt = rstd_of(C, x3, xt, TT)
        norm_apply(C, x3, xt, r, rt, g2, "g2s", h, "hT", TT)
        for f in range(NF):
            wg, wgt = wu.fetch(wup, 128 * f)
            wv, wvt = wu.fetch(wup, DFF + 128 * f)
            psg, psgt = psg_get()
            mm_fm(C, psg, psgt, wg, wgt, h, "hT", TT, 16)
            psv, psvt = psv_get()
            mm_fm(C, psv, psvt, wv, wvt, h, "hT", TT, 16)
            gb, gbt = gb_get()
            if ti == 0:
                psh, psht = psh_get()
                mm_fm(C, psh, psht, wg, wgt, hh, "hh", 2, 16)
                C.cp("act", gb[:, 0:2], psh[:, 0:2], reads=[psht], writes=[(gbt, 0)])
            else:
                C.cp("pool", gb[:, 0:2], gprev[:, f, :], reads=[("gprev", f)], writes=[(gbt, 0)])
            C.cp("act", gb[:, 2:TT + 2], psg[:, :], reads=[psgt], writes=[(gbt, 1)])
            C.cp("pool", gprev[:, f, :], gb[:, TT:TT + 2], reads=[(gbt, 1)], writes=[("gprev", f)])
            acc, acct = acc_get()
            C.ts("dve", acc[:, :], gb[:, 2:TT + 2], wcv[:, f, 2:3], bcv[:, f:f + 1], ALU.mult, ALU.add,
                 reads=[(gbt, 1), "wcvs", "bcvs"], writes=[acct])
            C.stt(acc[:, :], gb[:, 1:TT + 1], wcv[:, f, 1:2], acc[:, :], ALU.mult, ALU.add,
                  reads=[(gbt, 0), (gbt, 1), "wcvs", acct], writes=[acct])
            C.stt(acc[:, :], gb[:, 0:TT], wcv[:, f, 0:1], acc[:, :], ALU.mult, ALU.add,
                  reads=[(gbt, 0), (gbt, 1), "wcvs", acct], writes=[acct])
            sl, slt = sl_get()
            C.actv(sl[:, :], acc[:, :], AF.Silu, reads=[acct], writes=[slt])
            C.tt("dve", gT[:, f, :], sl[:, :], psv[:, :], ALU.mult, reads=[slt, psvt], writes=[("gT", f)])
        for dc in range(16):
            wdt, wdtt = wd.fetch(wdn, 128 * dc)
            ps, pst = psd_get()
            for f in range(NF):
                C.mm(ps[:, :], wdt[:, f, :], gT[:, f, :], f == 0, f == NF - 1, reads=[wdtt, ("gT", f)], writes=[pst])
            C.cp("act", y[:, dc, :], ps[:, :], reads=[pst], writes=[("yT", dc)])
        postnorm_residual(C, y, "yT", x3, xt, g3, "g3s", TT)
        store_x3(C, xo[:, :, t0:t0 + TT], x3, xt, "st_xo")
    return C.finish()


def build_P3():
    C = Ctx()
    P = C.P
    xT = C.din("xT", [16, 128, TOK])
    gk_d = C.din("gk", [128, 16])
    gq_d = C.din("gq", [128, 16])
    wkv = C.din("wkv", [16, 128, 3072])
    wq = C.din("wq", [16, 128, 2048])
    J_d = C.din("J", [128, 128], BF16)
    KRT = C.dout("KRT", [12, 128, TOK], BF16)
    VR = C.dout("VR", [TOK, 1536], BF16)
    qT = C.dout("qT", [12, 128, TOK], BF16)
    qmT = C.dout("qmT", [4, 128, TOK], BF16)
    setup_consts(C)
    gk = load_small(C, "gks", gk_d, [128, 16])
    gq = load_small(C, "gqs", gq_d, [128, 16])
    J = load_small(C, "Js", J_d, [128, 128], BF16)
    x_get = C.pool_of("x3", 2, [128, 16, TT], F32)
    hk_get = C.pool_of("hk", 1, [128, 16, TT], BF16)
    hq_get = C.pool_of("hq", 1, [128, 16, TT], BF16)
    wf = WStream(C, "wf", 3, 16, 128)
    wt_ = WStream(C, "wt", 2, 16, 512)
    ps_get = C.pool_of("psA", 3, [128, TT], F32, psum=True)
    pst_get = C.pool_of("psT", 2, [128, 512], BF16, psum=True)
    tm_get = C.pool_of("tm", 3, [128, 512], BF16)
    ko_get = C.pool_of("ko", 2, [128, 12, TT], BF16)
    vo_get = C.pool_of("vo", 3, [128, 512], BF16)
    qo_get = C.pool_of("qo", 3, [128, TT], BF16)
    for ti in range(TOK // TT):
        t0 = ti * TT
        x3, xt = x_get()
        load_x3(C, x3, xt, xT[:, :, t0:t0 + TT], xt)
        r, rt = rstd_of(C, x3, xt, TT)
        hk, hkt = hk_get()
        hq, hqt = hq_get()
        norm_apply(C, x3, xt, r, rt, gk, "gks", hk, hkt, TT)
        norm_apply(C, x3, xt, r, rt, gq, "gqs", hq, hqt, TT)
        ko, kot = ko_get()
        for g in range(3):
            wtile, wtok = wt_.fetch(wkv, 512 * g)
            for b in range(4):
                ps, pst = ps_get()
                for c in range(16):
                    C.mm(ps[:, :], hk[:, c, 128 * b:128 * b + 128], wtile[:, c, :], c == 0, c == 15, reads=[wtok, (hkt, c)], writes=[pst])
                tm, tmt = tm_get()
                C.cp("act", tm[:, :], ps[:, :], reads=[pst], writes=[tmt])
                pt, ptt = pst_get()
                for i in range(4):
                    C.tr(pt[:, 128 * i:128 * i + 128], tm[:, 128 * i:128 * i + 128], J[:, :], reads=[tmt, "Js"], writes=[ptt])
                for i in range(4):
                    C.cp("dve", ko[:, 4 * g + i, 128 * (3 - b):128 * (3 - b) + 128], pt[:, 128 * i:128 * i + 128], reads=[ptt],
                         writes=[(kot, 4 * g + i, b)])
        C.store("st_k", KRT[:, :, TOK - t0 - TT:TOK - t0].rearrange("h e t -> e h t"), ko[:, :, :],
                reads=[(kot, hh_, b) for hh_ in range(12) for b in range(4)])
        for g in range(3):
            wtile, wtok = wt_.fetch(wkv, 1536 + 512 * g)
            for b in range(4):
                ps, pst = ps_get()
                for c in range(16):
                    C.mm(ps[:, :], hk[:, c, 128 * b:128 * b + 128], wtile[:, c, :], c == 0, c == 15, reads=[wtok, (hkt, c)], writes=[pst])
                tm, tmt = tm_get()
                C.cp("act", tm[:, :], ps[:, :], reads=[pst], writes=[tmt])
                ps2, ps2t = ps_get()
                C.mm(ps2[:, :], J[:, :], tm[:, :], True, True, reads=["Js", tmt], writes=[ps2t])
                vo, vot = vo_get()
                C.cp("act", vo[:, :], ps2[:, :], reads=[ps2t], writes=[vot])
                r0 = TOK - t0 - 128 * (b + 1)
                C.store("st_v", VR[r0:r0 + 128, 512 * g:512 * g + 512], vo[:, :], reads=[vot])
        for hh_ in range(16):
            wtile, wtok = wf.fetch(wq, 128 * hh_)
            ps, pst = ps_get()
            mm_fm(C, ps, pst, wtile, wtok, hq, hqt, TT, 16)
            qo, qot = qo_get()
            C.actv(qo[:, :], ps[:, :], AF.Copy, reads=[pst], writes=[qot], scale=SCALE)
            dst = qT[hh_, :, t0:t0 + TT] if hh_ < 12 else qmT[hh_ - 12, :, t0:t0 + TT]
            C.store("st_q", dst, qo[:, :], reads=[qot])
    return C.finish()


def build_SB():
    C = Ctx()
    P = C.P
    KRT = C.din("KRT", [12, 128, S], BF16)
    VR = C.din("VR", [S, 1536], BF16)
    qT = C.din("qT", [12, 128, TOK], BF16)
    Mc_d = C.din("Mc", [128, 1024])
    I_d = C.din("I", [128, 128], BF16)
    oT = C.dout("oT", [12, 128, TOK], BF16)
    Mc = load_small(C, "Mcs", Mc_d, [128, 1024])
    I = load_small(C, "Is", I_d, [128, 128], BF16)
    onesf = C.sb("onesf", [128, TT], F32)
    C.memset("dve", onesf[:, :], 1.0, ["onesf"])
    NR = 8
    RK = S // NR
    Kres = C.sb("Kres", [128, S], BF16)
    Vres = C.sb("Vres", [128, S // 128, 128], BF16)
    q_get = C.pool_of("qh", 2, [128, TOK], BF16)
    o_get = C.pool_of("osb", 2, [128, TOK], BF16)
    g_get = C.pool_of("gsb", 3, [128, TT], F32)
    b_get = C.pool_of("cbuf", 3, [128, TT + 1], F32)
    a_get = C.pool_of("abf", 3, [128, TT], BF16)
    at_get = C.pool_of("atb", 3, [128, TT], BF16)
    psz_get = C.pool_of("psZ", 2, [128, TT], F32, psum=True)
    pst_get = C.pool_of("psT", 2, [128, TT], BF16, psum=True)
    pso_get = C.pool_of("psO", 2, [128, 128], F32, psum=True)
    for hh in range(12):
        for rg in range(NR):
            C.load("K%d" % rg, Kres[:, RK * rg:RK * (rg + 1)], KRT[hh, :, RK * rg:RK * (rg + 1)], writes=[("K", rg)])
            C.load("V%d" % rg, Vres[:, 16 * rg:16 * (rg + 1), :],
                   VR[RK * rg:RK * (rg + 1), 128 * hh:128 * hh + 128].rearrange("(b p) e -> p b e", p=128), writes=[("V", rg)])
        qh, qht = q_get()
        C.load(qht, qh[:, :], qT[hh, :, :], writes=[qht])
        osb, ot = o_get()
        for m in range(15, -1, -1):
            I0 = 15360 - 1024 * m
            nt = 2 * m + 2
            cb, cbt = b_get()
            C.memset("pool", cb[:, 0:1], 1.0, [(cbt, 0)])
            po, pot = pso_get()
            for kt in range(nt):
                i0 = I0 + TT * kt
                rg = i0 // RK
                pz, pzt = psz_get()
                C.mm(pz[:, :], qh[:, 128 * m:128 * m + 128], Kres[:, i0:i0 + TT], True, True, reads=[qht, ("K", rg)], writes=[pzt])
                gs, gst = g_get()
                C.actv(gs[:, :], pz[:, :], AF.Sigmoid, reads=[pzt], writes=[gst], scale=-1.0)
                if kt < 2:
                    C.tt("dve", gs[:, :], gs[:, :], Mc[:, TT * kt:TT * kt + TT], ALU.max, reads=[gst, "Mcs"], writes=[gst])
                P.dve(lambda e, cb=cb, gs=gs: e.tensor_tensor_scan(out=cb[:, 1:TT + 1], data0=gs[:, :], data1=onesf[:, :],
                                                                    initial=cb[:, 0:1], op0=ALU.mult, op1=ALU.mult),
                      reads=[gst, "onesf", (cbt, 0)], writes=[(cbt, 1)])
                ab, abt = a_get()
                C.tt("pool", ab[:, :], cb[:, 0:TT], cb[:, 1:TT + 1], ALU.subtract, reads=[(cbt, 0), (cbt, 1)], writes=[abt])
                if kt < nt - 1:
                    cb2, cbt2 = b_get()
                    C.cp("act", cb2[:, 0:1], cb[:, TT:TT + 1], reads=[(cbt, 1)], writes=[(cbt2, 0)])
                pt, ptt = pst_get()
                for i in range(4):
                    C.tr(pt[:, 128 * i:128 * i + 128], ab[:, 128 * i:128 * i + 128], I[:, :], reads=[abt, "Is"], writes=[ptt])
                at, att = at_get()
                C.cp("act", at[:, :], pt[:, :], reads=[ptt], writes=[att])
                for i in range(4):
                    C.mm(po[:, :], Vres[:, i0 // 128 + i, :], at[:, 128 * i:128 * i + 128], kt == 0 and i == 0, kt == nt - 1 and i == 3,
                         reads=[("V", rg), att], writes=[pot])
                if kt < nt - 1:
                    cb, cbt = cb2, cbt2
            C.cp("act", osb[:, 128 * m:128 * m + 128], po[:, :], reads=[pot], writes=[(ot, m)])
        C.store("st_o", oT[hh, :, :], osb[:, :], reads=[(ot, m) for m in range(16)])
    return C.finish()


def build_B2():
    C = Ctx()
    oT = C.din("oT", [12, 128, TOK], BF16)
    qmT = C.din("qmT", [4, 128, TOK], BF16)
    xT = C.din("xT", [16, 128, TOK])
    memT = C.din("memT", [16, 128, NMEM])
    g1_d = C.din("g1", [128, 16])
    g4_d = C.din("g4", [128, 16])
    wmkv = C.din("wmkv", [16, 128, 1024])
    wo_d = C.din("wo", [16, 128, D])
    xo = C.dout("xo", [16, 128, TOK])
    setup_consts(C)
    mkT, mv = mem_kv(C, memT, g4_d, wmkv)
    heads = C.sb("heads", [128, 16, TOK], BF16)
    for hh in range(12):
        C.load("ld_o%d" % (hh % 4), heads[:, hh, :], oT[hh, :, :], writes=[("heads", hh, ti) for ti in range(TOK // TT)])
    mem_attn(C, mkT, mv, qmT, heads, "heads", 12)
    out_proj(C, heads, "heads", 16, wo_d, xT, g1_d, xo)
    return C.finish()


def _fm(a):
    T, F = a.shape
    return np.ascontiguousarray(a.T).reshape(F // 128, 128, T)


def _gain(g):
    return np.ascontiguousarray(np.asarray(g, np.float32).reshape(16, 128).T)


def _run(nc, in_maps):
    res = run_bass_kernel_spmd(nc, in_maps, core_ids=list(range(NC)))
    return res.results


def _rope_tables(c):
    pos = (np.arange(2 * TOK, dtype=np.float32) + np.float32(TOK * c - TOK)).astype(np.float32)
    inv = (np.float32(10000.0) ** (-np.arange(0, HD, 2, dtype=np.float32) / np.float32(HD))).astype(np.float32)
    ang = pos[:, None] * inv[None, :]
    cos = np.cos(ang).T.astype(np.float32)
    sin = np.sin(ang).T.astype(np.float32)
    ck = np.concatenate([cos, cos], 0)
    sk = np.concatenate([-sin, sin], 0)
    return ck, sk, (ck * np.float32(SCALE)).astype(np.float32), (sk * np.float32(SCALE)).astype(np.float32)


def _ffn(layer, xm, norms, w_up, w_conv, b_conv, w_down):
    nc = build_F()
    wup = np.asarray(w_up[layer], np.float32).reshape(16, 128, 2 * DFF)
    wcv = np.ascontiguousarray(np.asarray(w_conv[layer], np.float32).reshape(3, NF, 128).transpose(2, 1, 0))
    bcv = np.ascontiguousarray(np.asarray(b_conv[layer], np.float32).reshape(NF, 128).T)
    wdn = np.asarray(w_down[layer], np.float32).reshape(NF, 128, D)
    g2 = _gain(norms[layer, 2])
    g3 = _gain(norms[layer, 3])
    maps = []
    for c in range(NC):
        xh = np.zeros((16, 128, 2), np.float32) if c == 0 else np.ascontiguousarray(xm[c - 1][:, :, TOK - 2:TOK])
        maps.append(dict(xT=xm[c], xh=xh, g2=g2, g3=g3, wup=wup, wcv=wcv, bcv=bcv, wdn=wdn))
    r = _run(nc, maps)
    return [np.asarray(r[c]["xo"]) for c in range(NC)]


def kernel(x, mem, norms, w_in_a, w_o_a, g_kv, w_kv, w_in_b, w_o_b, w_mem_kv, w_up, w_conv, b_conv, w_down):
    x2 = np.asarray(x, np.float32)[0]
    norms = np.asarray(norms, np.float32)
    memT = _fm(np.asarray(mem, np.float32)[0])
    xown = [_fm(x2[TOK * c:TOK * (c + 1)]) for c in range(NC)]
    nc = build_A0()
    w_in = np.asarray(w_in_a, np.float32)[0].reshape(16, 128, 5120)
    g0 = _gain(norms[0, 0])
    maps = []
    for c in range(NC):
        halo = np.zeros((16, 128, TOK), np.float32) if c == 0 else xown[c - 1]
        ck, sk, cq, sq = _rope_tables(c)
        maps.append(dict(xT=np.concatenate([halo, xown[c]], axis=2), w_in=w_in, g0=g0, cosk=ck, sink=sk, cosq=cq, sinq=sq))
    rA0 = _run(nc, maps)
    nc = build_A1()
    tri_cur = (np.arange(128)[None, :] >= np.arange(128)[:, None]).astype(np.float32)
    tri_prev = (np.arange(128)[:, None] >= np.arange(128)[None, :]).astype(np.float32)
    wmkv0 = np.asarray(w_mem_kv, np.float32)[0].reshape(16, 128, 1024)
    wo0 = np.asarray(w_o_a, np.float32)[0].reshape(8, 128, D)
    maps = []
    for c in range(NC):
        masks = np.zeros((128, 3, 256), np.float32)
        masks[:, 0, 0:128] = tri_prev
        masks[:, 0, 128:256] = tri_cur
        masks[:, 1, 0:128] = tri_prev if c > 0 else 0.0
        masks[:, 1, 128:256] = tri_cur
        maps.append(dict(qT=np.asarray(rA0[c]["qT"]), kT=np.asarray(rA0[c]["kT"]), v=np.asarray(rA0[c]["v"]),
                         qmT=np.asarray(rA0[c]["qmT"]), xT=xown[c], memT=memT, g1=_gain(norms[0, 1]), g4=_gain(norms[0, 4]),
                         wmkv=wmkv0, wo=wo0, masks=masks.astype(NPBF)))
    rA1 = _run(nc, maps)
    xm = [np.asarray(rA1[c]["xo"]) for c in range(NC)]
    del rA0, rA1
    x1 = _ffn(0, xm, norms, np.asarray(w_up, np.float32), np.asarray(w_conv, np.float32), np.asarray(b_conv, np.float32),
              np.asarray(w_down, np.float32))
    nc = build_P3()
    J = np.eye(128, dtype=np.float32)[::-1].astype(NPBF)
    wkv = np.asarray(w_kv, np.float32).reshape(16, 128, 3072)
    wq = np.asarray(w_in_b, np.float32)[0].reshape(16, 128, 2048)
    maps = [dict(xT=x1[c], gk=_gain(g_kv), gq=_gain(norms[1, 0]), wkv=wkv, wq=wq, J=J) for c in range(NC)]
    rP3 = _run(nc, maps)
    nc = build_SB()
    KRT = np.concatenate([np.asarray(rP3[c]["KRT"]) for c in range(NC - 1, -1, -1)], axis=2)
    VR = np.concatenate([np.asarray(rP3[c]["VR"]) for c in range(NC - 1, -1, -1)], axis=0)
    qall = np.concatenate([np.asarray(rP3[c]["qT"]) for c in range(NC)], axis=2)
    Ibf = np.eye(128, dtype=np.float32).astype(NPBF)
    maps = []
    for c in range(NC):
        qc = np.concatenate([qall[:, :, 128 * (8 * m + c):128 * (8 * m + c) + 128] for m in range(16)], axis=2)
        Mc = np.zeros((128, 1024), np.float32)
        for kb in range(8):
            if kb < 7 - c:
                Mc[:, 128 * kb:128 * kb + 128] = 1.0
            elif kb == 7 - c:
                Mc[:, 128 * kb:128 * kb + 128] = (np.arange(128)[None, :] <= 127 - np.arange(128)[:, None]).astype(np.float32)
        maps.append(dict(KRT=KRT, VR=VR, qT=np.ascontiguousarray(qc), Mc=Mc, I=Ibf))
    rSB = _run(nc, maps)
    osb = [np.asarray(rSB[c]["oT"]) for c in range(NC)]
    del rSB, KRT, VR
    nc = build_B2()
    wmkv1 = np.asarray(w_mem_kv, np.float32)[1].reshape(16, 128, 1024)
    wo1 = np.asarray(w_o_b, np.float32)[0].reshape(16, 128, D)
    maps = []
    for o in range(NC):
        oT = np.concatenate([osb[(16 * o + lb) % 8][:, :, 128 * ((16 * o + lb) // 8):128 * ((16 * o + lb) // 8) + 128]
                             for lb in range(16)], axis=2)
        maps.append(dict(oT=np.ascontiguousarray(oT), qmT=np.asarray(rP3[o]["qmT"]), xT=x1[o], memT=memT,
                         g1=_gain(norms[1, 1]), g4=_gain(norms[1, 4]), wmkv=wmkv1, wo=wo1))
    rB2 = _run(nc, maps)
    xm1 = [np.asarray(rB2[c]["xo"]) for c in range(NC)]
    x2o = _ffn(1, xm1, norms, np.asarray(w_up, np.float32), np.asarray(w_conv, np.float32), np.asarray(b_conv, np.float32),
               np.asarray(w_down, np.float32))
    out = np.empty((1, S, D), np.float32)
    for c in range(NC):
        out[0, TOK * c:TOK * (c + 1), :] = x2o[c].reshape(D, TOK).T
    return out
```

```python
import numpy as np
import ml_dtypes
from contextlib import ExitStack
import concourse.bass as bass
import concourse.mybir as mybir
from concourse.bass_utils import run_bass_kernel_spmd

F32 = mybir.dt.float32
BF16 = mybir.dt.bfloat16
I32 = mybir.dt.int32
AF = mybir.ActivationFunctionType
ALU = mybir.AluOpType
NPBF = ml_dtypes.bfloat16

D = 2048
S = 16384
NC = 8
TOK = S // NC
HD = 128
DFF = 5632
NF = DFF // 128
NMEM = 256
EPS = 1e-6
SCALE = HD ** -0.5
TT = 512
Q2_CS = 12 * 128 * 256
Q2_HS = 128 * 256

COMPUTE = ("pe", "act", "dve", "pool")


class Op:
    __slots__ = ("eng", "fn", "reads", "writes", "lane", "idx", "waits", "signal", "sigval", "is_dma", "inc", "phase")

    def __init__(self, eng, fn, reads, writes, lane, inc=16):
        self.inc = inc
        self.eng = eng
        self.fn = fn
        self.reads = reads
        self.writes = writes
        self.lane = lane
        self.is_dma = lane is not None
        self.waits = []
        self.signal = False
        self.sigval = None


class Prog:
    def __init__(self, nc, same_engine_sync=True):
        self.nc = nc
        self.ops = []
        self.last_w = {}
        self.readers = {}
        self.same_engine_sync = same_engine_sync
        self.pending = {}
        self.phase = 0
        self.last_eng = {}
        self.last_lane = {}
        self.regs = {}

    def barrier(self):
        base = set(self.last_eng.values()) | set(self.last_lane.values())
        for e in ("pe", "act", "dve", "pool", "sp"):
            self.pending[e] = set(base) | self.pending.get(e, set())

    def op(self, eng, fn, reads=(), writes=(), lane=None, inc=16):
        o = Op(eng, fn, tuple(reads), tuple(writes), lane, inc)
        o.idx = len(self.ops)
        o.phase = self.phase
        deps = set()
        if eng in self.pending:
            deps |= self.pending.pop(eng)
        self.last_eng[eng] = o.idx
        if lane is not None:
            self.last_lane[lane] = o.idx
        for t in o.reads:
            w = self.last_w.get(t)
            if w is not None:
                deps.add(w)
        for t in o.writes:
            w = self.last_w.get(t)
            if w is not None:
                deps.add(w)
            for r in self.readers.get(t, {}).values():
                deps.add(r)
        for t in o.writes:
            self.last_w[t] = o.idx
            self.readers[t] = {}
        rk = ("lane", lane) if lane is not None else ("eng", eng)
        for t in o.reads:
            self.readers.setdefault(t, {})[rk] = o.idx
        deps.discard(o.idx)
        o.waits = sorted(deps)
        self.ops.append(o)
        return o

    def pe(self, fn, reads=(), writes=()):
        return self.op("pe", fn, reads, writes)

    def act(self, fn, reads=(), writes=()):
        return self.op("act", fn, reads, writes)

    def dve(self, fn, reads=(), writes=()):
        return self.op("dve", fn, reads, writes)

    def pool(self, fn, reads=(), writes=()):
        return self.op("pool", fn, reads, writes)

    def dma(self, q, lane, fn, reads=(), writes=(), chain=False):
        if chain:
            ct = "chain:" + str(lane)
            reads = tuple(reads) + (ct,)
            writes = tuple(writes) + (ct,)
        return self.op(q, fn, reads, writes, lane=lane)

    def emit(self, final_wait_ops=()):
        nc = self.nc
        ops = self.ops

        def need_sync(prod, cons):
            if prod.is_dma or cons.is_dma:
                return True
            if prod.eng != cons.eng:
                return True
            if prod.eng == "pe":
                return False
            return self.same_engine_sync

        for o in ops:
            o.waits = [w for w in o.waits if need_sync(ops[w], o)]
            for w in o.waits:
                ops[w].signal = True
        for w in final_wait_ops:
            w.signal = True
        cnt = {}
        lanes = []
        for o in ops:
            if o.is_dma:
                key = ("lane", o.lane)
                if key not in cnt:
                    lanes.append(o.lane)
                cnt[key] = cnt.get(key, 0) + o.inc
                o.sigval = cnt[key]
                o.signal = True
            elif o.signal:
                key = ("eng", o.eng, o.phase)
                cnt[key] = cnt.get(key, 0) + 1
                o.sigval = cnt[key]
        with ExitStack() as st:
            sems = {}
            for key in sorted(k for k in cnt if k[0] == "eng"):
                sems[key] = st.enter_context(nc.semaphore("s_%s_%d" % (key[1], key[2])))
            for i, l in enumerate(lanes):
                sems[("lane", l)] = st.enter_context(nc.semaphore("l%d" % i))
            block = st.enter_context(nc.Block())

            def semkey(o):
                return ("lane", o.lane) if o.is_dma else ("eng", o.eng, o.phase)

            def run_engine(ename, eng):
                waited = {}
                for o in ops:
                    if o.eng != ename:
                        continue
                    need = {}
                    for w in o.waits:
                        p = ops[w]
                        k = semkey(p)
                        if p.sigval > need.get(k, 0):
                            need[k] = p.sigval
                    for k, v in need.items():
                        if waited.get(k, 0) >= v:
                            continue
                        eng.wait_ge(sems[k], v)
                        waited[k] = v
                    ins = o.fn(eng)
                    if o.signal:
                        ins.then_inc(sems[semkey(o)], o.inc if o.is_dma else 1)
                if ename == "sp":
                    need = {}
                    for p in final_wait_ops:
                        k = semkey(p)
                        need[k] = max(need.get(k, 0), p.sigval)
                    for k, v in need.items():
                        eng.wait_ge(sems[k], v)

            @block.tensor
            def _(e):
                run_engine("pe", e)

            @block.scalar
            def _(e):
                with e.register("dyn_act") as r:
                    self.regs["act"] = r
                    run_engine("act", e)

            @block.vector
            def _(e):
                run_engine("dve", e)

            @block.gpsimd
            def _(e):
                with e.register("dyn_pool") as r:
                    self.regs["pool"] = r
                    run_engine("pool", e)

            @block.sync
            def _(e):
                with e.register("dyn_sp") as r:
                    self.regs["sp"] = r
                    run_engine("sp", e)


class Ctx:
    def __init__(self):
        self.nc = bass.Bass("TRN2", target_bir_lowering=False)
        self.P = Prog(self.nc)
        self.st = ExitStack()
        self.outs = []
        self.rot = {}
        self.prefix = ""
        self.binds = {}
        self.ext_in = []

    def begin_phase(self, prefix, binds):
        self.prefix = prefix
        self.binds = binds
        self.rot = {}
        for k in ("na_get", "psm_get", "xo_get"):
            self.__dict__.pop(k, None)
        self.st = ExitStack()

    def end_phase(self):
        self.st.close()
        self.P.barrier()
        self.P.phase += 1

    def internal(self, name, shape, dt=F32):
        return self.nc.dram_tensor(name, list(shape), dt)

    def din(self, name, shape, dt=F32):
        if name in self.binds:
            return self.binds[name]
        nm = self.prefix + name
        self.ext_in.append(nm)
        return self.nc.dram_tensor(nm, list(shape), dt, kind="ExternalInput").ap()

    def dout(self, name, shape, dt=F32):
        if name in self.binds:
            return self.binds[name]
        return self.nc.dram_tensor(self.prefix + name, list(shape), dt, kind="ExternalOutput").ap()

    def sb(self, name, shape, dt):
        return self.st.enter_context(self.nc.sbuf_tensor(self.prefix + name, list(shape), dt))

    def ps(self, name, shape, dt=F32):
        return self.st.enter_context(self.nc.psum_tensor(self.prefix + name, list(shape), dt))

    def pool_of(self, name, n, shape, dt, psum=False):
        bufs = []
        for i in range(n):
            nm = "%s%d" % (name, i)
            bufs.append((self.ps(nm, shape, dt) if psum else self.sb(nm, shape, dt), nm))
        self.rot[name] = [bufs, 0]

        def get():
            b, i = self.rot[name]
            self.rot[name][1] = i + 1
            return b[i % len(b)]

        return get

    def store(self, lane, out_ap, in_ap, reads, q="sp"):
        t0 = reads[0]
        lane = "st:" + str(t0 if isinstance(t0, str) else t0[0])
        o = self.P.dma(q, lane, lambda e: e.dma_start(out=out_ap, in_=in_ap), reads=reads)
        self.outs.append(o)
        return o

    def load_dyn(self, lane, out_ap, tensor, offs_ap, pattern, writes, reads, chain=False, q="sp"):
        P = self.P

        def fn(e):
            r = P.regs[q]
            e.reg_load(r, offs_ap)
            return e.dma_start(out=out_ap, in_=bass.AP(tensor, r, pattern))

        return P.dma(q, lane, fn, reads=reads, writes=writes, chain=chain)

    def allgather(self, name, in_t, out_t):
        P = self.P
        P.barrier()
        o = P.op("pool", lambda e: e.collective_compute("AllGather", ALU.bypass, replica_groups=[list(range(NC))],
                                                        ins=[in_t.ap().opt()], outs=[out_t.ap().opt()]),
                 (), (), lane="cc_" + name, inc=1)
        P.barrier()
        return o

    def load(self, lane, out_ap, in_ap, writes, q="sp", reads=(), chain=False):
        return self.P.dma(q, lane, lambda e: e.dma_start(out=out_ap, in_=in_ap), writes=writes, reads=reads, chain=chain)

    def mm(self, out, lhsT, rhs, start, stop, reads, writes):
        return self.P.pe(lambda e: e.matmul(out, lhsT, rhs, start=start, stop=stop), reads=reads, writes=writes)

    def tr(self, out, in_, ident, reads, writes):
        return self.P.pe(lambda e: e.transpose(out, in_, ident), reads=reads, writes=writes)

    def actv(self, out, in_, func, reads, writes, scale=1.0, bias=None):
        if bias is None:
            return self.P.act(lambda e: e.activation(out=out, in_=in_, func=func, scale=scale), reads=reads, writes=writes)
        return self.P.act(lambda e: e.activation(out=out, in_=in_, func=func, scale=scale, bias=bias), reads=reads, writes=writes)

    def tt(self, eng, out, in0, in1, op, reads, writes):
        return self.P.op(eng, lambda e: e.tensor_tensor(out=out, in0=in0, in1=in1, op=op), reads, writes)

    def ts(self, eng, out, in0, s1, s2, op0, op1, reads, writes):
        if s2 is None:
            return self.P.op(eng, lambda e: e.tensor_scalar(out=out, in0=in0, scalar1=s1, scalar2=None, op0=op0), reads, writes)
        return self.P.op(eng, lambda e: e.tensor_scalar(out=out, in0=in0, scalar1=s1, scalar2=s2, op0=op0, op1=op1), reads, writes)

    def stt(self, out, in0, scalar, in1, op0, op1, reads, writes):
        return self.P.dve(lambda e: e.scalar_tensor_tensor(out=out, in0=in0, scalar=scalar, in1=in1, op0=op0, op1=op1),
                          reads=reads, writes=writes)

    def cp(self, eng, out, in_, reads, writes):
        if eng == "act":
            return self.P.act(lambda e: e.activation(out=out, in_=in_, func=AF.Copy), reads=reads, writes=writes)
        return self.P.op(eng, lambda e: e.tensor_copy(out=out, in_=in_), reads, writes)

    def recip(self, out, in_, reads, writes):
        return self.P.dve(lambda e: e.reciprocal(out=out, in_=in_), reads=reads, writes=writes)

    def memset(self, eng, ap, val, writes):
        return self.P.op(eng, lambda e: e.memset(ap, val), (), writes)

    def finish(self):
        self.P.emit(final_wait_ops=self.outs)
        _LAST_CTX["ext_in"] = list(self.ext_in) + ["offs", "cinfo", "memT", "xT"]
        return self.nc


def setup_consts(C):
    P = C.P
    ones = C.sb("ones_bf", [128, 128], BF16)
    P.dve(lambda e: e.memset(ones[:, :], 1.0), writes=["ones"])
    C.ones = ones
    epsc = C.sb("epsc", [128, 1], F32)
    P.dve(lambda e: e.memset(epsc[:, :], EPS), writes=["epsc"])
    C.epsc = epsc
    C.sq_get = C.pool_of("sq", 3, [128, TT], BF16)
    C.rstd_get = C.pool_of("rstd", 2, [128, TT], F32)
    C.ps_ss_get = C.pool_of("ps_ss", 1, [128, TT], F32, psum=True)


def rstd_of(C, src, srctok, T, nch=16):
    P = C.P
    ps, pst = C.ps_ss_get()
    for c in range(nch):
        sq, sqt = C.sq_get()
        P.act(lambda e, c=c, sq=sq: e.activation(out=sq[:, 0:T], in_=src[:, c, 0:T], func=AF.Square),
              reads=[(srctok, c)], writes=[sqt])
        P.pe(lambda e, c=c, sq=sq: e.matmul(ps[:, 0:T], C.ones[:, :], sq[:, 0:T], start=(c == 0), stop=(c == nch - 1)),
             reads=[sqt, "ones"], writes=[pst])
    r, rt = C.rstd_get()
    P.act(lambda e: e.activation(out=r[:, 0:T], in_=ps[:, 0:T], func=AF.Sqrt, scale=1.0 / D, bias=C.epsc[:, 0:1]),
          reads=[pst, "epsc"], writes=[rt])
    P.dve(lambda e: e.reciprocal(out=r[:, 0:T], in_=r[:, 0:T]), reads=[rt], writes=[rt])
    return r, rt


def norm_apply(C, x3, xtok, r, rt, gain, gtok, h, htok, T, nch=16):
    P = C.P
    if not hasattr(C, "na_get"):
        C.na_get = C.pool_of("natmp", 2, [128, TT], F32)
    for c in range(nch):
        if c % 2 == 0:
            P.dve(lambda e, c=c: e.scalar_tensor_tensor(out=h[:, c, 0:T], in0=x3[:, c, 0:T], scalar=gain[:, c:c + 1],
                                                       in1=r[:, 0:T], op0=ALU.mult, op1=ALU.mult),
                  reads=[(xtok, c), rt, gtok], writes=[(htok, c)])
        else:
            tmp, tt_ = C.na_get()
            P.act(lambda e, c=c, tmp=tmp: e.activation(out=tmp[:, 0:T], in_=x3[:, c, 0:T], func=AF.Copy, scale=gain[:, c:c + 1]),
                  reads=[(xtok, c), gtok], writes=[tt_])
            P.pool(lambda e, c=c, tmp=tmp: e.tensor_tensor(out=h[:, c, 0:T], in0=tmp[:, 0:T], in1=r[:, 0:T], op=ALU.mult),
                   reads=[tt_, rt], writes=[(htok, c)])


def postnorm_residual(C, y, ytok, x3, xtok, gain, gtok, T):
    P = C.P
    r, rt = rstd_of(C, y, ytok, T)
    for c in range(16):
        P.dve(lambda e, c=c: e.scalar_tensor_tensor(out=y[:, c, 0:T], in0=y[:, c, 0:T], scalar=gain[:, c:c + 1],
                                                     in1=r[:, 0:T], op0=ALU.mult, op1=ALU.mult),
              reads=[(ytok, c), rt, gtok], writes=[(ytok, c)])
        P.pool(lambda e, c=c: e.tensor_tensor(out=x3[:, c, 0:T], in0=x3[:, c, 0:T], in1=y[:, c, 0:T], op=ALU.add),
               reads=[(ytok, c), (xtok, c)], writes=[(xtok, c)])


def load_x3(C, x3, xtok, src_ap, lane):
    T = src_ap.shape[2]
    C.P.dma("sp", lane, lambda e: e.dma_start(out=x3[:, :, 0:T], in_=src_ap.rearrange("c p t -> p c t")),
            writes=[(xtok, c) for c in range(16)])


def store_x3(C, dst_ap, x3, xtok, lane):
    T = dst_ap.shape[2]
    C.store(lane, dst_ap.rearrange("c p t -> p c t"), x3[:, :, 0:T], reads=[(xtok, c) for c in range(16)])


def load_small(C, name, src_ap, shape, dt=F32, q="sp"):
    t = C.sb(name, shape, dt)
    idx = tuple(slice(None) for _ in shape)
    C.P.dma(q, "const", lambda e: e.dma_start(out=t[idx], in_=src_ap), writes=[name], chain=True)
    return t


class WStream:
    def __init__(self, C, name, nbuf, nk, ncols):
        self.C = C
        self.name = name
        self.get = C.pool_of(name, nbuf, [128, nk, ncols], BF16)
        self.nk = nk
        self.ncols = ncols

    def fetch(self, w_ap, col0):
        t, tok = self.get()
        src = w_ap[:, :, col0:col0 + self.ncols].rearrange("c p n -> p c n")
        self.C.P.dma("pool", tok, lambda e: e.dma_start(out=t[:, :, :], in_=src), writes=[tok])
        return t, tok


def mm_fm(C, ps, pst, wt, wtok, h, htok, T, nk, tsl=None):
    P = C.P
    for c in range(nk):
        rhs = h[:, c, 0:T] if tsl is None else h[:, c, tsl]
        P.pe(lambda e, c=c, rhs=rhs: e.matmul(ps[:, 0:T], wt[:, c, :], rhs, start=(c == 0), stop=(c == nk - 1)),
             reads=[wtok, (htok, c)], writes=[pst])


def emit_A0(C):
    P = C.P
    NT = 2 * TOK
    xT = C.din("xT", [16, 128, NT])
    w_in = C.din("w_in", [16, 128, 5120])
    g0_d = C.din("g0", [128, 16])
    cosk = C.din("cosk", [128, NT])
    sink = C.din("sink", [128, NT])
    cosq = C.din("cosq", [128, NT])
    sinq = C.din("sinq", [128, NT])
    kT = C.dout("kT", [12, 128, NT], BF16)
    v = C.dout("v", [NT, 1536], BF16)
    qT = C.dout("qT", [12, 128, TOK], BF16)
    qmT = C.dout("qmT", [4, 128, TOK], BF16)
    setup_consts(C)
    g0 = load_small(C, "g0s", g0_d, [128, 16])
    x_get = C.pool_of("x3", 2, [128, 16, TT], F32)
    h_get = C.pool_of("hT", 2, [128, 16, TT], BF16)
    wf = WStream(C, "wf", 3, 16, 128)
    wt_ = WStream(C, "wt", 2, 16, 512)
    tab_get = C.pool_of("tab", 2, [128, 4, TT], F32)
    ps_get = C.pool_of("psA", 3, [128, TT], F32, psum=True)
    t1_get = C.pool_of("t1", 2, [128, TT], F32)
    t2_get = C.pool_of("t2", 2, [128, TT], F32)
    ko_get = C.pool_of("ko", 3, [128, TT], BF16)
    vo_get = C.pool_of("vo", 3, [128, 512], BF16)

    def rope_evac(ps, pst, tab, tabt, ci, si, dst_ap, lane):
        t1, t1t = t1_get()
        t2, t2t = t2_get()
        ko, kot = ko_get()
        P.dve(lambda e: e.tensor_tensor(out=t1[:, :], in0=ps[:, :], in1=tab[:, ci, :], op=ALU.mult),
              reads=[pst, (tabt, ci)], writes=[t1t])
        P.dve(lambda e: e.tensor_tensor(out=t2[0:64, :], in0=ps[64:128, :], in1=tab[0:64, si, :], op=ALU.mult),
              reads=[pst, (tabt, si)], writes=[(t2t, 0)])
        P.dve(lambda e: e.tensor_tensor(out=t2[64:128, :], in0=ps[0:64, :], in1=tab[64:128, si, :], op=ALU.mult),
              reads=[pst, (tabt, si)], writes=[(t2t, 1)])
        P.pool(lambda e: e.tensor_tensor(out=ko[:, :], in0=t1[:, :], in1=t2[:, :], op=ALU.add),
               reads=[t1t, (t2t, 0), (t2t, 1)], writes=[kot])
        C.store(lane, dst_ap, ko[:, :], reads=[kot])

    for ti in range(NT // TT):
        t0 = ti * TT
        own = t0 >= TOK
        x3, xt = x_get()
        load_x3(C, x3, xt, xT[:, :, t0:t0 + TT], xt)
        tab, tabt = tab_get()
        for i, src in enumerate((cosk, sink, cosq, sinq)):
            if i >= 2 and not own:
                continue
            C.load(tabt, tab[:, i, :], src[:, t0:t0 + TT], writes=[(tabt, i)], chain=True)
        r, rt = rstd_of(C, x3, xt, TT)
        h, ht = h_get()
        norm_apply(C, x3, xt, r, rt, g0, "g0s", h, ht, TT)
        for hh in range(12):
            wtile, wtok = wf.fetch(w_in, 1536 + 128 * hh)
            ps, pst = ps_get()
            mm_fm(C, ps, pst, wtile, wtok, h, ht, TT, 16)
            rope_evac(ps, pst, tab, tabt, 0, 1, kT[hh, :, t0:t0 + TT], "st_k")
        for g in range(3):
            wtile, wtok = wt_.fetch(w_in, 3072 + 512 * g)
            for b in range(TT // 128):
                ps, pst = ps_get()
                for c in range(16):
                    C.mm(ps[:, :], h[:, c, 128 * b:128 * b + 128], wtile[:, c, :], c == 0, c == 15,
                         reads=[wtok, (ht, c)], writes=[pst])
                vo, vot = vo_get()
                P.act(lambda e, ps=ps, vo=vo: e.activation(out=vo[:, :], in_=ps[:, :], func=AF.Copy),
                      reads=[pst], writes=[vot])
                C.store("st_v", v[t0 + 128 * b:t0 + 128 * b + 128, 512 * g:512 * g + 512], vo[:, :], reads=[vot])
        if own:
            for hh in range(12):
                wtile, wtok = wf.fetch(w_in, 128 * hh)
                ps, pst = ps_get()
                mm_fm(C, ps, pst, wtile, wtok, h, ht, TT, 16)
                rope_evac(ps, pst, tab, tabt, 2, 3, qT[hh, :, t0 - TOK:t0 - TOK + TT], "st_q")
            for hh in range(4):
                wtile, wtok = wf.fetch(w_in, 4608 + 128 * hh)
                ps, pst = ps_get()
                mm_fm(C, ps, pst, wtile, wtok, h, ht, TT, 16)
                ko, kot = ko_get()
                P.act(lambda e, ps=ps, ko=ko: e.activation(out=ko[:, :], in_=ps[:, :], func=AF.Copy, scale=SCALE),
                      reads=[pst], writes=[kot])
                C.store("st_qm", qmT[hh, :, t0 - TOK:t0 - TOK + TT], ko[:, :], reads=[kot])
    C.end_phase()


def mem_kv(C, memT_d, g4_d, wmkv_d):
    C.xo_get = C.pool_of("xo3", 1, [128, 16, TT], F32)
    m3, m3t = C.xo_get()
    load_x3(C, m3, m3t, memT_d, m3t)
    g4 = load_small(C, "g4s", g4_d, [128, 16])
    r, rt = rstd_of(C, m3, m3t, NMEM)
    mh = C.sb("mh", [128, 16, NMEM], BF16)
    norm_apply(C, m3, m3t, r, rt, g4, "g4s", mh, "mh", NMEM)
    mkT = C.sb("mkT", [128, 4, NMEM], BF16)
    mv = C.sb("mv", [128, 2, 512], BF16)
    wf = WStream(C, "wmf", 2, 16, 128)
    wt_ = WStream(C, "wmt", 1, 16, 512)
    psm_get = C.pool_of("psM", 2, [128, TT], F32, psum=True)
    C.psm_get = psm_get
    for hh in range(4):
        wtile, wtok = wf.fetch(wmkv_d, 128 * hh)
        ps, pst = psm_get()
        mm_fm(C, ps, pst, wtile, wtok, mh, "mh", NMEM, 16)
        C.cp("act", mkT[:, hh, :], ps[:, 0:NMEM], reads=[pst], writes=["mkT"])
    wtile, wtok = wt_.fetch(wmkv_d, 512)
    for b in range(2):
        ps, pst = psm_get()
        for c in range(16):
            C.mm(ps[:, :], mh[:, c, 128 * b:128 * b + 128], wtile[:, c, :], c == 0, c == 15,
                 reads=[wtok, ("mh", c)], writes=[pst])
        C.cp("act", mv[:, b, :], ps[:, :], reads=[pst], writes=["mv"])
    return mkT, mv


def mem_attn(C, mkT, mv, qmT_d, heads, htok, hc0):
    qm_get = C.pool_of("qmt", 2, [128, TT], BF16)
    pm_get = C.pool_of("pm", 2, [128, 2, TT], BF16)
    rd_get = C.pool_of("rdm", 2, [128, TT], F32)
    for ti in range(TOK // TT):
        t0 = ti * TT
        for hh in range(4):
            qm, qmt = qm_get()
            C.load(qmt, qm[:, :], qmT_d[hh, :, t0:t0 + TT], writes=[qmt])
            pm, pmt = pm_get()
            for mc in range(2):
                ps, pst = C.psm_get()
                C.mm(ps[:, :], mkT[:, hh, 128 * mc:128 * mc + 128], qm[:, :], True, True, reads=["mkT", qmt], writes=[pst])
                C.actv(pm[:, mc, :], ps[:, :], AF.Exp, reads=[pst], writes=[(pmt, mc)])
            pso, psot = C.psm_get()
            for mc in range(2):
                C.mm(pso[:, :], mv[:, mc, 128 * hh:128 * hh + 128], pm[:, mc, :], mc == 0, mc == 1,
                     reads=["mv", (pmt, mc)], writes=[psot])
            psd, psdt = C.psm_get()
            for mc in range(2):
                C.mm(psd[:, :], C.ones[:, :], pm[:, mc, :], mc == 0, mc == 1, reads=["ones", (pmt, mc)], writes=[psdt])
            rd, rdt = rd_get()
            C.recip(rd[:, :], psd[:, :], reads=[psdt], writes=[rdt])
            C.tt("dve", heads[:, hc0 + hh, t0:t0 + TT], pso[:, :], rd[:, :], ALU.mult, reads=[psot, rdt],
                 writes=[(htok, hc0 + hh, ti)])


def out_proj(C, heads, htok, nhc, wo_d, x_d, g1_d, xo_d):
    g1 = load_small(C, "g1s", g1_d, [128, 16])
    wos = WStream(C, "wos", 3, nhc, 128)
    y_get = C.pool_of("yT", 1, [128, 16, TT], F32)
    for ti in range(TOK // TT):
        t0 = ti * TT
        x3, xt = C.xo_get()
        load_x3(C, x3, xt, x_d[:, :, t0:t0 + TT], xt)
        y, yt = y_get()
        for dc in range(16):
            ps, pst = C.psm_get()
            wo, wot = wos.fetch(wo_d, 128 * dc)
            for hc in range(nhc):
                C.mm(ps[:, :], wo[:, hc, :], heads[:, hc, t0:t0 + TT], hc == 0, hc == nhc - 1,
                     reads=[wot, (htok, hc, ti)], writes=[pst])
            C.cp("act", y[:, dc, :], ps[:, :], reads=[pst], writes=[(yt, dc)])
        postnorm_residual(C, y, yt, x3, xt, g1, "g1s", TT)
        store_x3(C, xo_d[:, :, t0:t0 + TT], x3, xt, "st_xo")


def emit_A1(C):
    P = C.P
    NT = 2 * TOK
    qT = C.din("qT", [12, 128, TOK], BF16)
    kT = C.din("kT", [12, 128, NT], BF16)
    v = C.din("v", [NT, 1536], BF16)
    qmT = C.din("qmT", [4, 128, TOK], BF16)
    xT = C.din("xT", [16, 128, TOK])
    memT = C.din("memT", [16, 128, NMEM])
    g1_d = C.din("g1", [128, 16])
    g4_d = C.din("g4", [128, 16])
    wmkv = C.din("wmkv", [16, 128, 1024])
    wo_d = C.din("wo", [8, 128, D])
    masks_d = C.din("masks", [128, 3, 256], BF16)
    xo = C.dout("xo", [16, 128, TOK])
    setup_consts(C)
    masks = load_small(C, "masks_s", masks_d, [128, 3, 256], BF16)
    mkT, mv = mem_kv(C, memT, g4_d, wmkv)
    heads = C.sb("heads", [128, 8, TOK], BF16)
    num = C.sb("num", [128, TOK], F32)
    den = C.sb("den", [128, TOK], F32)
    k_get = C.pool_of("kbuf", 2, [128, NT], BF16)
    q_get = C.pool_of("qbuf", 2, [128, TOK], BF16)
    v_get = C.pool_of("vbuf", 4, [128, 2, 128], BF16)
    e_get = C.pool_of("ebuf", 3, [128, 256], BF16)
    pss_get = C.pool_of("psS", 2, [128, 256], F32, psum=True)
    pso_get = C.pool_of("psO", 2, [128, 256], F32, psum=True)
    for j in range(4):
        for g, d in enumerate((1, 4, 16)):
            hh = 4 * g + j
            kb, kbt = k_get()
            C.load(kbt, kb[:, :], kT[hh, :, :], writes=[kbt])
            qb, qbt = q_get()
            C.load(qbt, qb[:, :], qT[hh, :, :], writes=[qbt])
            for n in range(16 // d):
                for r in range(d):
                    bq = 128 * d * n + r
                    qs = slice(bq, bq + 127 * d + 1, d)
                    kc = slice(TOK + bq, TOK + bq + 127 * d + 1, d)
                    kp = slice(TOK + bq - 128 * d, TOK + bq - 128 * d + 127 * d + 1, d)
                    vb, vbt = v_get()
                    C.load(vbt, vb[:, 0, :], v[kp, 128 * hh:128 * hh + 128], writes=[(vbt, 0)], chain=True)
                    C.load(vbt, vb[:, 1, :], v[kc, 128 * hh:128 * hh + 128], writes=[(vbt, 1)], chain=True)
                    ps, pst = pss_get()
                    C.mm(ps[:, 0:128], kb[:, kp], qb[:, qs], True, True, reads=[kbt, qbt], writes=[(pst, 0)])
                    C.mm(ps[:, 128:256], kb[:, kc], qb[:, qs], True, True, reads=[kbt, qbt], writes=[(pst, 1)])
                    eb, ebt = e_get()
                    C.actv(eb[:, :], ps[:, :], AF.Exp, reads=[(pst, 0), (pst, 1)], writes=[ebt])
                    mi = 1 if n == 0 else 0
                    C.tt("dve", eb[:, :], eb[:, :], masks[:, mi, :], ALU.mult, reads=[ebt, "masks_s"], writes=[ebt])
                    po, pot = pso_get()
                    C.mm(po[:, 0:128], vb[:, 0, :], eb[:, 0:128], True, False, reads=[(vbt, 0), ebt], writes=[(pot, 0)])
                    C.mm(po[:, 0:128], vb[:, 1, :], eb[:, 128:256], False, True, reads=[(vbt, 1), ebt], writes=[(pot, 0)])
                    C.mm(po[:, 128:256], C.ones[:, :], eb[:, 0:128], True, False, reads=["ones", ebt], writes=[(pot, 1)])
                    C.mm(po[:, 128:256], C.ones[:, :], eb[:, 128:256], False, True, reads=["ones", ebt], writes=[(pot, 1)])
                    ut = ("nd", n, r) if d == 1 else "nd_all"
                    if g == 0:
                        C.cp("dve", num[:, qs], po[:, 0:128], reads=[(pot, 0)], writes=["num"])
                        C.cp("dve", den[:, qs], po[:, 128:256], reads=[(pot, 1)], writes=["den"])
                    else:
                        C.tt("dve", num[:, qs], num[:, qs], po[:, 0:128], ALU.add, reads=[(pot, 0), "num"], writes=["num"])
                        C.tt("dve", den[:, qs], den[:, qs], po[:, 128:256], ALU.add, reads=[(pot, 1), "den"], writes=["den"])
        C.recip(den[:, :], den[:, :], reads=["den"], writes=["den"])
        for ti in range(TOK // TT):
            C.tt("dve", heads[:, j, ti * TT:(ti + 1) * TT], num[:, ti * TT:(ti + 1) * TT], den[:, ti * TT:(ti + 1) * TT], ALU.mult,
                 reads=["num", "den"], writes=[("heads", j, ti)])
    mem_attn(C, mkT, mv, qmT, heads, "heads", 4)
    out_proj(C, heads, "heads", 8, wo_d, xT, g1_d, xo)
    C.end_phase()


def emit_F(C):
    P = C.P
    xT = C.din("xT", [16, 128, TOK])
    g2_d = C.din("g2", [128, 16])
    g3_d = C.din("g3", [128, 16])
    wup = C.din("wup", [16, 128, 2 * DFF])
    wcv_d = C.din("wcv", [128, NF, 3])
    bcv_d = C.din("bcv", [128, NF])
    wdn = C.din("wdn", [NF, 128, D])
    xo = C.dout("xo", [16, 128, TOK])
    setup_consts(C)
    g2 = load_small(C, "g2s", g2_d, [128, 16])
    g3 = load_small(C, "g3s", g3_d, [128, 16])
    wcv = load_small(C, "wcvs", wcv_d, [128, NF, 3])
    bcv = load_small(C, "bcvs", bcv_d, [128, NF])
    x_get = C.pool_of("x3", 1, [128, 16, TT], F32)
    h = C.sb("hT", [128, 16, TT], BF16)
    y = C.sb("yT", [128, 16, TT], F32)
    gT = C.sb("gT", [128, NF, TT], BF16)
    xh3 = C.sb("xh3", [128, 16, 2], F32)
    hh = C.sb("hh", [128, 16, 2], BF16)
    gprev = C.sb("gprev", [128, NF, 2], F32)
    wu = WStream(C, "wu", 6, 16, 128)
    wd = WStream(C, "wd", 2, NF, 128)
    psg_get = C.pool_of("psG", 2, [128, TT], F32, psum=True)
    psv_get = C.pool_of("psV", 2, [128, TT], F32, psum=True)
    psh_get = C.pool_of("psH", 1, [128, 2], F32, psum=True)
    psd_get = C.pool_of("psD", 2, [128, TT], F32, psum=True)
    gb_get = C.pool_of("gbuf", 2, [128, TT + 2], F32)
    acc_get = C.pool_of("acc", 2, [128, TT], F32)
    sl_get = C.pool_of("silu", 2, [128, TT], F32)
    offs = load_small(C, "offs", C.binds["offs"], [1, 64], I32)
    cinfo = load_small(C, "cinfo", C.binds["cinfo"], [128, 4])
    hbig = gT
    hb = C.sb("hbig", [128, 16, 128], F32)
    C.load_dyn("const", hb[:, :, :], C.binds["G_t"], offs[0:1, 0:1], [[128, 128], [128 * 128, 16], [1, 128]],
               writes=["hbig"], reads=["offs"], chain=True)
    C.cp("dve", xh3[:, :, :], hb[:, :, 126:128], reads=["hbig"], writes=[("xh3", c) for c in range(16)])
    for c in range(16):
        C.ts("dve", xh3[:, c, :], xh3[:, c, :], cinfo[:, 0:1], None, ALU.mult, None, reads=[("xh3", c), "cinfo"], writes=[("xh3", c)])
    r, rt = rstd_of(C, xh3, "xh3", 2)
    norm_apply(C, xh3, "xh3", r, rt, g2, "g2s", hh, "hh", 2)
    for ti in range(TOK // TT):
        t0 = ti * TT
        x3, xt = x_get()
        load_x3(C, x3, xt, xT[:, :, t0:t0 + TT], xt)
        r, rt = rstd_of(C, x3, xt, TT)
        norm_apply(C, x3, xt, r, rt, g2, "g2s", h, "hT", TT)
        for f in range(NF):
            wg, wgt = wu.fetch(wup, 128 * f)
            wv, wvt = wu.fetch(wup, DFF + 128 * f)
            psg, psgt = psg_get()
            mm_fm(C, psg, psgt, wg, wgt, h, "hT", TT, 16)
            psv, psvt = psv_get()
            mm_fm(C, psv, psvt, wv, wvt, h, "hT", TT, 16)
            gb, gbt = gb_get()
            if ti == 0:
                psh, psht = psh_get()
                mm_fm(C, psh, psht, wg, wgt, hh, "hh", 2, 16)
                C.cp("act", gb[:, 0:2], psh[:, 0:2], reads=[psht], writes=[(gbt, 0)])
            else:
                C.cp("pool", gb[:, 0:2], gprev[:, f, :], reads=[("gprev", f)], writes=[(gbt, 0)])
            C.cp("act", gb[:, 2:TT + 2], psg[:, :], reads=[psgt], writes=[(gbt, 1)])
            C.cp("pool", gprev[:, f, :], gb[:, TT:TT + 2], reads=[(gbt, 1)], writes=[("gprev", f)])
            acc, acct = acc_get()
            C.ts("dve", acc[:, :], gb[:, 2:TT + 2], wcv[:, f, 2:3], bcv[:, f:f + 1], ALU.mult, ALU.add,
                 reads=[(gbt, 1), "wcvs", "bcvs"], writes=[acct])
            C.stt(acc[:, :], gb[:, 1:TT + 1], wcv[:, f, 1:2], acc[:, :], ALU.mult, ALU.add,
                  reads=[(gbt, 0), (gbt, 1), "wcvs", acct], writes=[acct])
            C.stt(acc[:, :], gb[:, 0:TT], wcv[:, f, 0:1], acc[:, :], ALU.mult, ALU.add,
                  reads=[(gbt, 0), (gbt, 1), "wcvs", acct], writes=[acct])
            sl, slt = sl_get()
            C.actv(sl[:, :], acc[:, :], AF.Silu, reads=[acct], writes=[slt])
            C.tt("dve", gT[:, f, :], sl[:, :], psv[:, :], ALU.mult, reads=[slt, psvt], writes=[("gT", f)])
        for dc in range(16):
            wdt, wdtt = wd.fetch(wdn, 128 * dc)
            ps, pst = psd_get()
            for f in range(NF):
                C.mm(ps[:, :], wdt[:, f, :], gT[:, f, :], f == 0, f == NF - 1, reads=[wdtt, ("gT", f)], writes=[pst])
            C.cp("act", y[:, dc, :], ps[:, :], reads=[pst], writes=[("yT", dc)])
        postnorm_residual(C, y, "yT", x3, xt, g3, "g3s", TT)
        store_x3(C, xo[:, :, t0:t0 + TT], x3, xt, "st_xo")
    C.end_phase()


def emit_P3(C):
    P = C.P
    xT = C.din("xT", [16, 128, TOK])
    gk_d = C.din("gk", [128, 16])
    gq_d = C.din("gq", [128, 16])
    wkv = C.din("wkv", [16, 128, 3072])
    wq = C.din("wq", [16, 128, 2048])
    J_d = C.din("J", [128, 128], BF16)
    KRT = C.dout("KRT", [12, 128, TOK], BF16)
    VR = C.dout("VR", [TOK, 1536], BF16)
    qmT = C.dout("qmT", [4, 128, TOK], BF16)
    setup_consts(C)
    gk = load_small(C, "gks", gk_d, [128, 16])
    gq = load_small(C, "gqs", gq_d, [128, 16])
    J = load_small(C, "Js", J_d, [128, 128], BF16)
    x_get = C.pool_of("x3", 2, [128, 16, TT], F32)
    hk_get = C.pool_of("hk", 1, [128, 16, TT], BF16)
    hq_get = C.pool_of("hq", 1, [128, 16, TT], BF16)
    wf = WStream(C, "wf", 3, 16, 128)
    wt_ = WStream(C, "wt", 2, 16, 512)
    ps_get = C.pool_of("psA", 3, [128, TT], F32, psum=True)
    pst_get = C.pool_of("psT", 2, [128, 512], BF16, psum=True)
    tm_get = C.pool_of("tm", 3, [128, 512], BF16)
    ko_get = C.pool_of("ko", 2, [128, 12, TT], BF16)
    vo_get = C.pool_of("vo", 3, [128, 512], BF16)
    qo_get = C.pool_of("qo", 3, [128, TT], BF16)
    for ti in range(TOK // TT):
        t0 = ti * TT
        x3, xt = x_get()
        load_x3(C, x3, xt, xT[:, :, t0:t0 + TT], xt)
        r, rt = rstd_of(C, x3, xt, TT)
        hk, hkt = hk_get()
        hq, hqt = hq_get()
        norm_apply(C, x3, xt, r, rt, gk, "gks", hk, hkt, TT)
        norm_apply(C, x3, xt, r, rt, gq, "gqs", hq, hqt, TT)
        ko, kot = ko_get()
        for g in range(3):
            wtile, wtok = wt_.fetch(wkv, 512 * g)
            for b in range(4):
                ps, pst = ps_get()
                for c in range(16):
                    C.mm(ps[:, :], hk[:, c, 128 * b:128 * b + 128], wtile[:, c, :], c == 0, c == 15, reads=[wtok, (hkt, c)], writes=[pst])
                tm, tmt = tm_get()
                C.cp("act", tm[:, :], ps[:, :], reads=[pst], writes=[tmt])
                pt, ptt = pst_get()
                for i in range(4):
                    C.tr(pt[:, 128 * i:128 * i + 128], tm[:, 128 * i:128 * i + 128], J[:, :], reads=[tmt, "Js"], writes=[ptt])
                for i in range(4):
                    C.cp("dve", ko[:, 4 * g + i, 128 * (3 - b):128 * (3 - b) + 128], pt[:, 128 * i:128 * i + 128], reads=[ptt],
                         writes=[(kot, 4 * g + i, b)])
        C.store("st_k", KRT[:, :, TOK - t0 - TT:TOK - t0].rearrange("h e t -> e h t"), ko[:, :, :],
                reads=[(kot, hh_, b) for hh_ in range(12) for b in range(4)])
        for g in range(3):
            wtile, wtok = wt_.fetch(wkv, 1536 + 512 * g)
            for b in range(4):
                ps, pst = ps_get()
                for c in range(16):
                    C.mm(ps[:, :], hk[:, c, 128 * b:128 * b + 128], wtile[:, c, :], c == 0, c == 15, reads=[wtok, (hkt, c)], writes=[pst])
                tm, tmt = tm_get()
                C.cp("act", tm[:, :], ps[:, :], reads=[pst], writes=[tmt])
                ps2, ps2t = ps_get()
                C.mm(ps2[:, :], J[:, :], tm[:, :], True, True, reads=["Js", tmt], writes=[ps2t])
                vo, vot = vo_get()
                C.cp("act", vo[:, :], ps2[:, :], reads=[ps2t], writes=[vot])
                r0 = TOK - t0 - 128 * (b + 1)
                C.store("st_v", VR[r0:r0 + 128, 512 * g:512 * g + 512], vo[:, :], reads=[vot])
        for hh_ in range(16):
            wtile, wtok = wf.fetch(wq, 128 * hh_)
            ps, pst = ps_get()
            mm_fm(C, ps, pst, wtile, wtok, hq, hqt, TT, 16)
            qo, qot = qo_get()
            C.actv(qo[:, :], ps[:, :], AF.Copy, reads=[pst], writes=[qot], scale=SCALE)
            if hh_ < 12:
                dst = bass.AP(C.binds["kvq_t"], 1536 * 2048 + 4 * (ti % 2) * Q2_CS + hh_ * Q2_HS + (ti // 2) * 128,
                              [[256, 128], [Q2_CS, 4], [1, 128]])
                C.store("st_q", dst, qo[:, :].rearrange("p (j t) -> p j t", j=4), reads=[qot])
            else:
                C.store("st_q", qmT[hh_ - 12, :, t0:t0 + TT], qo[:, :], reads=[qot])
    C.end_phase()


def emit_SB(C):
    P = C.P
    KVQ = C.binds["KVQ_t"]
    KVQa = KVQ.ap()
    RS = 4608 * 2048
    Mc_d = C.din("Mc", [128, 1024])
    I_d = C.din("I", [128, 128], BF16)
    oT = C.dout("oT", [12, 128, TOK], BF16)
    offs = load_small(C, "offs", C.binds["offs"], [1, 64], I32)
    Mc = load_small(C, "Mcs", Mc_d, [128, 1024])
    I = load_small(C, "Is", I_d, [128, 128], BF16)
    W = 1024
    NB = W // 128
    onesf = C.sb("onesf", [128, W], F32)
    C.memset("dve", onesf[:, :], 1.0, ["onesf"])
    NR = 8
    RK = S // NR
    psz_get = C.pool_of("psZ", 2, [128, W], F32, psum=True)
    pst_get = C.pool_of("psT", 2, [128, W], BF16, psum=True)
    pso_get = C.pool_of("psO", 2, [128, 128], F32, psum=True)
    Kres = C.sb("Kres", [128, S], BF16)
    Vres = C.sb("Vres", [128, S // 128, 128], BF16)
    q_get = C.pool_of("qh", 2, [128, TOK], BF16)
    o_get = C.pool_of("osb", 2, [128, TOK], BF16)
    g_get = C.pool_of("gsb", 3, [128, W], F32)
    b_get = C.pool_of("cbuf", 3, [128, W + 1], F32)
    a_get = C.pool_of("abf", 3, [128, W], BF16)
    at_get = C.pool_of("atb", 3, [128, W], BF16)
    for hh in range(12):
        for rg in range(NR):
            rk = 7 - rg
            C.load("KV%d" % rg, Kres[:, RK * rg:RK * (rg + 1)], KVQa[rk * 4608 + 128 * hh:rk * 4608 + 128 * hh + 128, :],
                   writes=[("K", rg)], chain=True)
            C.load("KV%d" % rg, Vres[:, 16 * rg:16 * (rg + 1), :],
                   bass.AP(KVQ, rk * RS + 3072 * 2048 + 128 * hh, [[1536, 128], [128 * 1536, 16], [1, 128]]),
                   writes=[("V", rg)], chain=True)
        qh, qht = q_get()
        C.load_dyn(qht, qh[:, :].rearrange("p (o x) -> p o x", o=8), KVQ, offs[0:1, 1 + hh:2 + hh],
                   [[256, 128], [RS, 8], [1, 256]], writes=[qht], reads=["offs"])
        osb, ot = o_get()
        for m in range(15, -1, -1):
            nt = m + 1
            po, pot = pso_get()
            pcb = None
            for kt in range(nt):
                i0 = 15360 - 1024 * m + W * kt
                rg = i0 // RK
                pz, pzt = psz_get()
                for j in range(W // TT):
                    C.mm(pz[:, TT * j:TT * j + TT], qh[:, 128 * m:128 * m + 128], Kres[:, i0 + TT * j:i0 + TT * j + TT], True, True,
                         reads=[qht, ("K", rg)], writes=[(pzt, j)])
                gs, gst = g_get()
                C.actv(gs[:, :], pz[:, :], AF.Sigmoid, reads=[(pzt, j) for j in range(W // TT)], writes=[gst], scale=-1.0)
                if kt == 0:
                    C.tt("dve", gs[:, :], gs[:, :], Mc[:, :], ALU.max, reads=[gst, "Mcs"], writes=[gst])
                cb, cbt = b_get()
                if pcb is None:
                    C.memset("pool", cb[:, 0:1], 1.0, [(cbt, 0)])
                else:
                    C.cp("pool", cb[:, 0:1], pcb[0][:, W:W + 1], reads=[(pcb[1], 1)], writes=[(cbt, 0)])
                C.P.dve(lambda e, cb=cb, gs=gs: e.tensor_tensor_scan(out=cb[:, 1:W + 1], data0=gs[:, :], data1=onesf[:, :],
                                                                    initial=cb[:, 0:1], op0=ALU.mult, op1=ALU.mult),
                        reads=[gst, "onesf", (cbt, 0)], writes=[(cbt, 1)])
                ab, abt = a_get()
                C.tt("pool", ab[:, :], cb[:, 0:W], cb[:, 1:W + 1], ALU.subtract, reads=[(cbt, 0), (cbt, 1)], writes=[abt])
                pt, ptt = pst_get()
                for i in range(NB):
                    C.tr(pt[:, 128 * i:128 * i + 128], ab[:, 128 * i:128 * i + 128], I[:, :], reads=[abt, "Is"], writes=[ptt])
                at, att = at_get()
                C.cp("act", at[:, :], pt[:, :], reads=[ptt], writes=[att])
                for i in range(NB):
                    C.mm(po[:, :], Vres[:, i0 // 128 + i, :], at[:, 128 * i:128 * i + 128], kt == 0 and i == 0,
                         kt == nt - 1 and i == NB - 1, reads=[("V", rg), att], writes=[pot])
                pcb = (cb, cbt)
            C.cp("act", osb[:, 128 * m:128 * m + 128], po[:, :], reads=[pot], writes=[(ot, m)])
        C.store("st_o", oT[hh, :, :], osb[:, :], reads=[(ot, m) for m in range(16)])
    C.end_phase()


def emit_B2(C):
    qmT = C.din("qmT", [4, 128, TOK], BF16)
    xT = C.din("xT", [16, 128, TOK])
    memT = C.din("memT", [16, 128, NMEM])
    g1_d = C.din("g1", [128, 16])
    g4_d = C.din("g4", [128, 16])
    wmkv = C.din("wmkv", [16, 128, 1024])
    wo_d = C.din("wo", [16, 128, D])
    xo = C.dout("xo", [16, 128, TOK])
    setup_consts(C)
    mkT, mv = mem_kv(C, memT, g4_d, wmkv)
    heads = C.sb("heads", [128, 16, TOK], BF16)
    offs = load_small(C, "offs", C.binds["offs"], [1, 64], I32)
    for hh in range(12):
        for u in range(2):
            qq = "sp" if (2 * hh + u) < 9 else "pool"
            C.load_dyn("ldo_" + qq, heads[:, hh, 1024 * u:1024 * u + 1024].rearrange("p (w t) -> p w t", w=8), C.binds["OALL_t"],
                       offs[0:1, 13 + 2 * hh + u:14 + 2 * hh + u], [[2048, 128], [1536 * 2048, 8], [1, 128]],
                       writes=[("heads", hh, 2 * u), ("heads", hh, 2 * u + 1)], reads=["offs"], chain=True, q=qq)
    mem_attn(C, mkT, mv, qmT, heads, "heads", 12)
    out_proj(C, heads, "heads", 16, wo_d, xT, g1_d, xo)
    C.end_phase()


def build_fused(upto=99):
    C = Ctx()
    nc = C.nc
    bf3 = lambda name, a, b, c: nc.dram_tensor(name, [a, b, c], BF16)
    offs_d = nc.dram_tensor("offs", [1, 64], I32, kind="ExternalInput").ap()
    cinfo_d = nc.dram_tensor("cinfo", [128, 4], F32, kind="ExternalInput").ap()
    memT = nc.dram_tensor("memT", [16, 128, NMEM], F32, kind="ExternalInput").ap()
    xT = nc.dram_tensor("xT", [16, 128, 2 * TOK], F32, kind="ExternalInput").ap()
    kT0 = bf3("i_kT0", 12, 128, 2 * TOK).ap()
    v0 = nc.dram_tensor("i_v0", [2 * TOK, 1536], BF16).ap()
    qT0 = bf3("i_qT0", 12, 128, TOK).ap()
    qmT0 = bf3("i_qmT0", 4, 128, TOK).ap()
    xm0 = nc.dram_tensor("i_xm0", [16, 128, TOK], F32).ap()
    x1 = nc.dram_tensor("i_x1", [16, 128, TOK], F32).ap()
    xm1 = nc.dram_tensor("i_xm1", [16, 128, TOK], F32).ap()
    h_in = [nc.dram_tensor("i_hin%d" % i, [2048, 128], F32) for i in range(2)]
    G = [nc.dram_tensor("i_G%d" % i, [NC * 2048, 128], F32) for i in range(2)]
    kvq_in = nc.dram_tensor("i_kvq", [4608, 2048], BF16)
    KVQ = nc.dram_tensor("i_KVQ", [NC * 4608, 2048], BF16)
    o_in = nc.dram_tensor("i_oin", [1536, 2048], BF16)
    OALL = nc.dram_tensor("i_OALL", [NC * 1536, 2048], BF16)
    qmT1 = bf3("i_qmT1", 4, 128, TOK).ap()
    kvq_a = kvq_in.ap()

    def dump(name, t_ap):
        shp = list(t_ap.shape)
        d = nc.dram_tensor("dbg_" + name, shp, t_ap.dtype, kind="ExternalOutput").ap()
        o = C.P.dma("sp", "const", lambda e: e.dma_start(out=d, in_=t_ap), chain=True)
        C.outs.append(o)

    def stop(k, dumps):
        if upto != k:
            return False
        C.P.barrier()
        for n, t in dumps:
            dump(n, t)
        return True

    C.begin_phase("a0_", dict(xT=xT, kT=kT0, v=v0, qT=qT0, qmT=qmT0))
    emit_A0(C)
    C.begin_phase("a1_", dict(xT=xT[:, :, TOK:2 * TOK], kT=kT0, v=v0, qT=qT0, qmT=qmT0, memT=memT, xo=xm0))
    emit_A1(C)
    if stop(1, [("xm0", xm0)]):
        return C.finish()

    def halo_exchange(i, xsrc):
        C.P.dma("sp", "const", lambda e: e.dma_start(out=h_in[i].ap().rearrange("(c p) t -> c p t", p=128),
                                                      in_=xsrc[:, :, TOK - 128:TOK]), chain=True)
        C.allgather("g%d" % i, h_in[i], G[i])

    halo_exchange(0, xm0)
    if stop(2, [("G0", G[0].ap())]):
        return C.finish()
    C.begin_phase("f0_", dict(xT=xm0, xo=x1, offs=offs_d, cinfo=cinfo_d, G_t=G[0]))
    emit_F(C)
    if stop(3, [("x1", x1)]):
        return C.finish()
    C.begin_phase("p3_", dict(xT=x1,
                              KRT=kvq_a[0:1536, :].rearrange("(h e) t -> h e t", e=128),
                              kvq_t=kvq_in,
                              VR=bass.AP(kvq_in, 3072 * 2048, [[1536, TOK], [1, 1536]]),
                              qmT=qmT1))
    emit_P3(C)
    if stop(4, [("kvq", kvq_a)]):
        return C.finish()
    C.allgather("kvq", kvq_in, KVQ)
    if stop(5, [("KVQ", KVQ.ap())]):
        return C.finish()
    C.begin_phase("sb_", dict(KVQ_t=KVQ, offs=offs_d, oT=o_in.ap().rearrange("(h e) t -> h e t", e=128)))
    emit_SB(C)
    if stop(6, [("oin", o_in.ap())]):
        return C.finish()
    C.allgather("o", o_in, OALL)
    C.begin_phase("b2_", dict(OALL_t=OALL, offs=offs_d, qmT=qmT1, xT=x1, memT=memT, xo=xm1))
    emit_B2(C)
    if stop(7, [("xm1", xm1)]):
        return C.finish()
    halo_exchange(1, xm1)
    C.begin_phase("f1_", dict(xT=xm1, offs=offs_d, cinfo=cinfo_d, G_t=G[1]))
    emit_F(C)
    return C.finish()


def _fm(a):
    T, F = a.shape
    return np.ascontiguousarray(a.T).reshape(F // 128, 128, T)


def _gain(g):
    return np.ascontiguousarray(np.asarray(g, np.float32).reshape(16, 128).T)


def _rope_tables(c):
    pos = (np.arange(2 * TOK, dtype=np.float32) + np.float32(TOK * c - TOK)).astype(np.float32)
    inv = (np.float32(10000.0) ** (-np.arange(0, HD, 2, dtype=np.float32) / np.float32(HD))).astype(np.float32)
    ang = pos[:, None] * inv[None, :]
    cos = np.cos(ang).T.astype(np.float32)
    sin = np.sin(ang).T.astype(np.float32)
    ck = np.concatenate([cos, cos], 0)
    sk = np.concatenate([-sin, sin], 0)
    return ck, sk, (ck * np.float32(SCALE)).astype(np.float32), (sk * np.float32(SCALE)).astype(np.float32)


_UPTO = 99
_LAST_CTX = {}


def nc_input_names(nc):
    return _LAST_CTX["ext_in"]


def kernel(x, mem, norms, w_in_a, w_o_a, g_kv, w_kv, w_in_b, w_o_b, w_mem_kv, w_up, w_conv, b_conv, w_down):
    f = lambda a: np.asarray(a, np.float32)
    x2 = f(x)[0]
    norms = f(norms)
    w_up, w_conv, b_conv, w_down, w_mem_kv = f(w_up), f(w_conv), f(b_conv), f(w_down), f(w_mem_kv)
    memT = _fm(f(mem)[0])
    xown = [_fm(x2[TOK * c:TOK * (c + 1)]) for c in range(NC)]
    nc = build_fused(_UPTO)
    tri_cur = (np.arange(128)[None, :] >= np.arange(128)[:, None]).astype(np.float32)
    tri_prev = (np.arange(128)[:, None] >= np.arange(128)[None, :]).astype(np.float32)
    shared = {
        "memT": memT,
        "a0_w_in": f(w_in_a)[0].reshape(16, 128, 5120), "a0_g0": _gain(norms[0, 0]),
        "a1_g1": _gain(norms[0, 1]), "a1_g4": _gain(norms[0, 4]),
        "a1_wmkv": w_mem_kv[0].reshape(16, 128, 1024), "a1_wo": f(w_o_a)[0].reshape(8, 128, D),
        "p3_gk": _gain(g_kv), "p3_gq": _gain(norms[1, 0]), "p3_wkv": f(w_kv).reshape(16, 128, 3072),
        "p3_wq": f(w_in_b)[0].reshape(16, 128, 2048), "p3_J": np.eye(128, dtype=np.float32)[::-1].astype(NPBF),
        "sb_I": np.eye(128, dtype=np.float32).astype(NPBF),
        "b2_g1": _gain(norms[1, 1]), "b2_g4": _gain(norms[1, 4]),
        "b2_wmkv": w_mem_kv[1].reshape(16, 128, 1024), "b2_wo": f(w_o_b)[0].reshape(16, 128, D),
    }
    for l, p in ((0, "f0_"), (1, "f1_")):
        shared[p + "g2"] = _gain(norms[l, 2])
        shared[p + "g3"] = _gain(norms[l, 3])
        shared[p + "wup"] = w_up[l].reshape(16, 128, 2 * DFF)
        shared[p + "wcv"] = np.ascontiguousarray(w_conv[l].reshape(3, NF, 128).transpose(2, 1, 0))
        shared[p + "bcv"] = np.ascontiguousarray(b_conv[l].reshape(NF, 128).T)
        shared[p + "wdn"] = w_down[l].reshape(NF, 128, D)
    maps = []
    for c in range(NC):
        m = dict(shared)
        halo = np.zeros((16, 128, TOK), np.float32) if c == 0 else xown[c - 1]
        m["xT"] = np.concatenate([halo, xown[c]], axis=2)
        ck, sk, cq, sq = _rope_tables(c)
        m["a0_cosk"], m["a0_sink"], m["a0_cosq"], m["a0_sinq"] = ck, sk, cq, sq
        masks = np.zeros((128, 3, 256), np.float32)
        masks[:, 0, 0:128] = tri_prev
        masks[:, 0, 128:256] = tri_cur
        masks[:, 1, 0:128] = tri_prev if c > 0 else 0.0
        masks[:, 1, 128:256] = tri_cur
        m["a1_masks"] = masks.astype(NPBF)
        Mc = np.zeros((128, 1024), np.float32)
        for kb in range(8):
            if kb < 7 - c:
                Mc[:, 128 * kb:128 * kb + 128] = 1.0
            elif kb == 7 - c:
                Mc[:, 128 * kb:128 * kb + 128] = (np.arange(128)[None, :] <= 127 - np.arange(128)[:, None]).astype(np.float32)
        m["sb_Mc"] = Mc
        offs = np.zeros((1, 64), np.int32)
        offs[0, 0] = max(c - 1, 0) * 2048 * 128
        for h in range(12):
            offs[0, 1 + h] = 1536 * 2048 + c * Q2_CS + h * Q2_HS
            for u in range(2):
                offs[0, 13 + 2 * h + u] = h * 128 * 2048 + 256 * c + 128 * u
        m["offs"] = offs
        cinfo = np.zeros((128, 4), np.float32)
        cinfo[:, 0] = 0.0 if c == 0 else 1.0
        m["cinfo"] = cinfo
        maps.append(m)
    if _UPTO != 99:
        names = set(nc_input_names(nc))
        maps = [{k: v for k, v in m.items() if k in names} for m in maps]
    res = run_bass_kernel_spmd(nc, maps, core_ids=list(range(NC)))
    if _UPTO != 99:
        return res
    out = np.empty((1, S, D), np.float32)
    for c in range(NC):
        out[0, TOK * c:TOK * (c + 1), :] = np.asarray(res.results[c]["f1_xo"]).reshape(D, TOK).T
    return out
```
